# Optimizing a Trainium2 kernel written in Bass

```python
import jax, jax.numpy as jnp
from jax import lax
import numpy as np

D_MODEL = 1024
BATCH = 4
SEQ = 8192
DEPTH = 2

MIX_WIDTH = D_MODEL
POOL_WINDOWS = (2, 4, 8, 16)
POOL_GROUPS = len(POOL_WINDOWS)
POOL_WIDTH = D_MODEL // 4
POOL_GROUP_DIM = POOL_WIDTH // POOL_GROUPS
SGU_CHUNK = 128
SGU_HEADS = 4
SGU_WIDTH = D_MODEL // 2
SGU_HEAD_DIM = SGU_WIDTH // SGU_HEADS
FNET_HEADS = 4
FNET_WIDTH = D_MODEL // 4
FNET_HEAD_DIM = FNET_WIDTH // FNET_HEADS
IN_PROJ_WIDTH = POOL_WIDTH + 2 * SGU_WIDTH + FNET_WIDTH
MEM_LEN = 256
XA_HEADS = 4
XA_HEAD_DIM = D_MODEL // XA_HEADS
FFN_HIDDEN = -(-8 * D_MODEL // (3 * 256)) * 256
RMS_EPS = 1e-6
LN_EPS = 1e-5

kernel_name = "hybrid_pool_sgu_fourier_encoder"


def _rmsnorm(x, g):
    xf = x.astype(jnp.float32)
    y = xf * lax.rsqrt(jnp.mean(xf * xf, axis=-1, keepdims=True) + RMS_EPS)
    return (y * g.astype(jnp.float32)).astype(x.dtype)


def _layernorm(x, g):
    xf = x.astype(jnp.float32)
    mu = jnp.mean(xf, axis=-1, keepdims=True)
    xc = xf - mu
    y = xc * lax.rsqrt(jnp.mean(xc * xc, axis=-1, keepdims=True) + LN_EPS)
    return (y * g.astype(jnp.float32)).astype(x.dtype)


def _pool_mixer(xa, w, scale):
    b, s, _ = xa.shape
    xf = xa.astype(jnp.float32)
    csum = jnp.concatenate(
        [jnp.zeros((b, 1, POOL_WIDTH), jnp.float32), lax.cumsum(xf, axis=1)], axis=1)
    t = jnp.arange(s)
    pooled = []
    for g, win in enumerate(POOL_WINDOWS):
        left = win // 2
        right = win - 1 - left
        lo = jnp.clip(t - left, 0, s - 1)
        hi = jnp.clip(t + right, 0, s - 1)
        cg = csum[..., g * POOL_GROUP_DIM:(g + 1) * POOL_GROUP_DIM]
        window_sum = jnp.take(cg, hi + 1, axis=1) - jnp.take(cg, lo, axis=1)
        count = (hi - lo + 1).astype(jnp.float32)[None, :, None]
        pooled.append(window_sum / count)
    pooled = jnp.stack(pooled, axis=2)
    diff = (pooled - xf.reshape(b, s, POOL_GROUPS, POOL_GROUP_DIM)).astype(xa.dtype)
    y = jnp.einsum('bsgc,gcd->bsgd', diff, w).reshape(b, s, POOL_WIDTH)
    return y * scale


def _spatial_gating(u, v, v_gain, w_s, b_s):
    b, s, _ = u.shape
    v = _layernorm(v, v_gain)
    vc = v.reshape(b, s // SGU_CHUNK, SGU_CHUNK, SGU_HEADS, SGU_HEAD_DIM)
    mixed = jnp.einsum('hpq,bnqhc->bnphc', w_s, vc) + jnp.transpose(b_s)[None, None, :, :, None]
    return u * mixed.reshape(b, s, SGU_WIDTH)


def _fourier_mixer(xc, w):
    b, s, _ = xc.shape
    xh = xc.astype(jnp.float32).reshape(b, s, FNET_HEADS, FNET_HEAD_DIM)
    f = jnp.fft.fftn(xh, axes=(1, 3), norm='ortho').real.astype(xc.dtype)
    return jnp.einsum('bshc,hcd->bshd', f, w).reshape(b, s, FNET_WIDTH)


def _memory_attention(h, m, wq, wk, wv, wo):
    b, s, _ = h.shape
    q = (h @ wq).reshape(b, s, XA_HEADS, XA_HEAD_DIM)
    k = (m @ wk).reshape(b, MEM_LEN, XA_HEADS, XA_HEAD_DIM)
    v = (m @ wv).reshape(b, MEM_LEN, XA_HEADS, XA_HEAD_DIM)
    scores = jnp.einsum('bshd,bmhd->bhsm', q, k).astype(jnp.float32) * (XA_HEAD_DIM ** -0.5)
    p = jax.nn.softmax(scores, axis=-1).astype(h.dtype)
    o = jnp.einsum('bhsm,bmhd->bshd', p, v).reshape(b, s, D_MODEL)
    return o @ wo


def _swiglu(h, wg, wu, wd):
    return (jax.nn.silu(h @ wg) * (h @ wu)) @ wd


def setup_inputs(seed: int = 0) -> dict:
    key = jax.random.key(seed)
    ks = jax.random.split(key, 32)
    L, D = DEPTH, D_MODEL

    def nrm(k, shape, fan_in):
        return jax.random.normal(k, shape, jnp.float32) * (fan_in ** -0.5)

    def gain(k, shape):
        return 1.0 + 0.05 * jax.random.normal(k, shape, jnp.float32)

    return {
        "x": jax.random.normal(ks[0], (BATCH, SEQ, D), jnp.float32),
        "mem": jax.random.normal(ks[1], (BATCH, MEM_LEN, D), jnp.float32),
        "ln_mix_pre": gain(ks[2], (L, D)),
        "w_in": nrm(ks[3], (L, D, IN_PROJ_WIDTH), D),
        "pool_w": nrm(ks[4], (L, POOL_GROUPS, POOL_GROUP_DIM, POOL_GROUP_DIM), POOL_GROUP_DIM),
        "pool_scale": gain(ks[5], (L, POOL_WIDTH)),
        "sgu_norm": gain(ks[6], (L, SGU_WIDTH)),
        "sgu_w": nrm(ks[7], (L, SGU_HEADS, SGU_CHUNK, SGU_CHUNK), SGU_CHUNK),
        "sgu_b": gain(ks[8], (L, SGU_HEADS, SGU_CHUNK)),
        "fnet_w": nrm(ks[9], (L, FNET_HEADS, FNET_HEAD_DIM, FNET_HEAD_DIM), FNET_HEAD_DIM),
        "w_out": nrm(ks[10], (L, MIX_WIDTH, D), MIX_WIDTH),
        "ln_mix_post": gain(ks[11], (L, D)),
        "ln_xa_pre": gain(ks[12], (L, D)),
        "ln_mem": gain(ks[13], (L, D)),
        "xa_wq": nrm(ks[14], (L, D, D), D),
        "xa_wk": nrm(ks[15], (L, D, D), D),
        "xa_wv": nrm(ks[16], (L, D, D), D),
        "xa_wo": nrm(ks[17], (L, D, D), D),
        "ln_xa_post": gain(ks[18], (L, D)),
        "ln_ffn_pre": gain(ks[19], (L, D)),
        "ffn_wg": nrm(ks[20], (L, D, FFN_HIDDEN), D),
        "ffn_wu": nrm(ks[21], (L, D, FFN_HIDDEN), D),
        "ffn_wd": nrm(ks[22], (L, FFN_HIDDEN, D), FFN_HIDDEN),
        "ln_ffn_post": gain(ks[23], (L, D)),
    }


def reference(x, mem, ln_mix_pre, w_in, pool_w, pool_scale, sgu_norm, sgu_w, sgu_b,
              fnet_w, w_out, ln_mix_post, ln_xa_pre, ln_mem, xa_wq, xa_wk, xa_wv,
              xa_wo, ln_xa_post, ln_ffn_pre, ffn_wg, ffn_wu, ffn_wd, ln_ffn_post):
    s_a = POOL_WIDTH
    s_u = s_a + SGU_WIDTH
    s_v = s_u + SGU_WIDTH
    for l in range(DEPTH):
        h = _rmsnorm(x, ln_mix_pre[l])
        z = h @ w_in[l]
        za = z[..., :s_a]
        zu = z[..., s_a:s_u]
        zv = z[..., s_u:s_v]
        zc = z[..., s_v:]
        ya = _pool_mixer(za, pool_w[l], pool_scale[l])
        yb = _spatial_gating(zu, zv, sgu_norm[l], sgu_w[l], sgu_b[l])
        yc = _fourier_mixer(zc, fnet_w[l])
        y = jnp.concatenate([ya, yb, yc], axis=-1) @ w_out[l]
        x = x + _rmsnorm(y, ln_mix_post[l])
        m = _rmsnorm(mem, ln_mem[l])
        h = _rmsnorm(x, ln_xa_pre[l])
        y = _memory_attention(h, m, xa_wq[l], xa_wk[l], xa_wv[l], xa_wo[l])
        x = x + _rmsnorm(y, ln_xa_post[l])
        h = _rmsnorm(x, ln_ffn_pre[l])
        y = _swiglu(h, ffn_wg[l], ffn_wu[l], ffn_wd[l])
        x = x + _rmsnorm(y, ln_ffn_post[l])
    return x
```

```python
import os
from contextlib import ExitStack
import numpy as np
import ml_dtypes
import concourse.bass as bass
import concourse.mybir as mybir
from concourse.bass_utils import run_bass_kernel_spmd

F32 = mybir.dt.float32
BF16 = mybir.dt.bfloat16
ALU = mybir.AluOpType
AF = mybir.ActivationFunctionType

D = 1024
S = 8192
OWN = 4096
L = 2
T = 512
KC = 8
FF = 2816
HC = 22
MEM = 256
RMS_EPS = 1e-6
LN_EPS = 1e-5
NG = 7
G_MIXPRE, G_MIXPOST, G_XAPRE, G_MEM, G_XAPOST, G_FFNPRE, G_FFNPOST = range(7)
NBIG = 13
U_AZ, U_U, U_V, U_WOUT, U_WQ, U_WK, U_WV, U_WO = 0, 1, 2, 3, 5, 7, 9, 11
FFT_SCALE = float(1.0 / np.sqrt(S * 64.0))
DBG_CUT = [99]


class Prog:
    def __init__(self):
        self.ops = []
        self.res = {}
        self.floor = None
        self.last_eng = {}
        self.last_dma = {}

    def add(self, eng, fn, reads=(), writes=(), dma=None):
        oid = len(self.ops)
        deps = set()
        if self.floor is not None:
            deps.add(self.floor)
        for r in reads:
            st = self.res.get(r)
            if st is not None and st[0] is not None:
                deps.add(st[0])
        for w in writes:
            st = self.res.get(w)
            if st is not None:
                if st[0] is not None:
                    deps.add(st[0])
                deps.update(st[1])
        for r in reads:
            self.res.setdefault(r, [None, []])[1].append(oid)
        for w in writes:
            self.res[w] = [oid, []]
        deps.discard(oid)
        self.ops.append(dict(eng=eng, fn=fn, deps=deps, dma=dma, marked=False, waits=[]))
        if dma is None:
            self.last_eng[eng] = oid
        else:
            self.last_dma[dma] = oid
        return oid

    def barrier(self, eng, fn):
        deps = set(self.last_eng.values()) | set(v for k, v in self.last_dma.items() if not k.startswith("cast"))
        oid = len(self.ops)
        if self.floor is not None:
            deps.add(self.floor)
        self.ops.append(dict(eng=eng, fn=fn, deps=deps, dma=None, marked=False, waits=[]))
        self.last_eng[eng] = oid
        self.floor = oid
        self.res = {k: v for k, v in self.res.items() if isinstance(k, tuple) and k[0] in ("wb", "wgu", "wdr", "wsm")}
        return oid

    def finalize(self):
        ops = self.ops
        for op in ops:
            for d in sorted(op["deps"]):
                a = ops[d]
                if a["eng"] == "pe" and op["eng"] == "pe" and a["dma"] is None and op["dma"] is None:
                    continue
                a["marked"] = True
                op["waits"].append(d)
        cnt = {}
        for op in ops:
            if op["dma"] is not None:
                op["marked"] = True
            if op["marked"]:
                key = ("dma", op["dma"]) if op["dma"] is not None else ("eng", op["eng"])
                inc = 16 if op["dma"] is not None else 1
                cnt[key] = cnt.get(key, 0) + inc
                op["sem"] = key
                op["val"] = cnt[key]
                op["inc"] = inc
        return sorted(cnt.keys(), key=str)

    def emit_engine(self, eng_name, eng, sems):
        waited = {}
        ops = self.ops
        for op in ops:
            if op["eng"] != eng_name:
                continue
            need = {}
            for d in op["waits"]:
                a = ops[d]
                if a["val"] > need.get(a["sem"], 0):
                    need[a["sem"]] = a["val"]
            for k in sorted(need, key=str):
                if waited.get(k, 0) >= need[k]:
                    continue
                eng.wait_ge(sems[k], need[k])
                waited[k] = need[k]
            ins = op["fn"](eng)
            if op["marked"]:
                ins.then_inc(sems[op["sem"]], op["inc"])
                if op["sem"] == ("eng", eng_name):
                    pass


def _bf(a):
    return np.ascontiguousarray(a.astype(ml_dtypes.bfloat16))


def _const_tables(half):
    off = OWN * half
    a = np.arange(64)[:, None].astype(np.float64)
    ka = np.arange(64)[None, :].astype(np.float64)
    ang = 2 * np.pi * a * ka / 64.0
    d64 = np.concatenate([np.cos(ang), -np.sin(ang)], axis=1)
    b = np.arange(128, dtype=np.int64)[:, None, None]
    kaa = np.arange(64, dtype=np.int64)[None, :, None]

    def etab(nkb):
        kb = np.arange(nkb, dtype=np.int64)[None, None, :]
        kloc = kaa + 64 * kb
        k = (kloc + off) % S
        sig = np.where((k * half) % 2 == 1, -1.0, 1.0)
        th = 2 * np.pi * ((k * b) % S).astype(np.float64) / S
        ec = sig * np.cos(th)
        es = sig * np.sin(th)
        e = np.empty((128, 64, 2, 2 * nkb), np.float64)
        e[:, :, 0, :nkb] = ec
        e[:, :, 0, nkb:] = -es
        e[:, :, 1, :nkb] = es
        e[:, :, 1, nkb:] = ec
        return e
    e0 = etab(128)
    e1 = etab(64)
    c = np.arange(64)[:, None].astype(np.float64)
    j = np.arange(64)[None, :].astype(np.float64)
    ph = 2 * np.pi * c * j / 64.0
    cd = np.zeros((128, 2, 128), np.float64)
    for hh in range(2):
        cd[hh * 64:(hh + 1) * 64, 0, hh * 64:(hh + 1) * 64] = np.cos(ph)
        cd[hh * 64:(hh + 1) * 64, 1, hh * 64:(hh + 1) * 64] = np.sin(ph)
    kinds = np.zeros((3, 3, 4, 128, 128), np.float64)
    for kind in range(3):
        for g, w in enumerate((2, 4, 8, 16)):
            left = w // 2
            right = w - 1 - left
            for tt in range(128):
                lo = tt - left
                hi = tt + right
                if kind == 1:
                    lo = max(lo, 0)
                if kind == 2:
                    hi = min(hi, 127)
                cntv = hi - lo + 1
                for sr in range(lo, hi + 1):
                    r = 0 if sr < 0 else (1 if sr < 128 else 2)
                    kinds[kind, r, g, sr % 128, tt] += 1.0 / cntv
                kinds[kind, 1, g, tt, tt] -= 1.0
    if half == 0:
        sel = [0, 1, 0, 0, 2]
    else:
        sel = [0, 0, 2, 1, 0]
    pb = np.stack([kinds[s_] for s_ in sel], axis=0)
    pb = pb.transpose(3, 0, 1, 2, 4).reshape(128, 60 * 128)
    return dict(d64=_bf(d64), e0=_bf(e0.reshape(128, -1)), e1=_bf(e1.reshape(128, -1)),
                cdft=_bf(cd.reshape(128, 256)), pb=_bf(pb))


def _big_unit(w, c0):
    return w[:, c0:c0 + 512].reshape(8, 128, 512).transpose(1, 0, 2).reshape(128, 4096)


def _prep_weights(inp):
    f = lambda k: np.asarray(inp[k], dtype=np.float32)
    w_in, w_out = f("w_in"), f("w_out")
    wq, wk, wv, wo = f("xa_wq"), f("xa_wk"), f("xa_wv"), f("xa_wo")
    wg, wu, wd = f("ffn_wg"), f("ffn_wu"), f("ffn_wd")
    wbig = np.zeros((L, NBIG, 128, 4096), np.float32)
    wgu = np.zeros((L, HC, 128, 2, 8, 128), np.float32)
    wdr = np.zeros((L, 8, 128, HC, 128), np.float32)
    wsm = np.zeros((L, 128, 1024), np.float32)
    for l in range(L):
        az = np.concatenate([w_in[l][:, 0:256], w_in[l][:, 1280:1536]], axis=1)
        wbig[l, U_AZ] = _big_unit(az, 0)
        wbig[l, U_U] = _big_unit(w_in[l], 256)
        wbig[l, U_V] = _big_unit(w_in[l], 768)
        for hf in range(2):
            wbig[l, U_WOUT + hf] = _big_unit(w_out[l], 512 * hf)
            wbig[l, U_WQ + hf] = _big_unit(wq[l], 512 * hf)
            wbig[l, U_WK + hf] = _big_unit(wk[l], 512 * hf)
            wbig[l, U_WV + hf] = _big_unit(wv[l], 512 * hf)
            wbig[l, U_WO + hf] = _big_unit(wo[l], 512 * hf)
        wgu[l, :, :, 0] = wg[l].reshape(8, 128, HC, 128).transpose(2, 1, 0, 3)
        wgu[l, :, :, 1] = wu[l].reshape(8, 128, HC, 128).transpose(2, 1, 0, 3)
        wdr[l] = wd[l].reshape(HC, 128, 8, 128).transpose(2, 1, 0, 3)
        pw, fw = f("pool_w")[l], f("fnet_w")[l]
        for ch in range(2):
            for gg in range(2):
                wsm[l, gg * 64:(gg + 1) * 64, ch * 128 + gg * 64: ch * 128 + (gg + 1) * 64] = pw[2 * ch + gg]
                wsm[l, gg * 64:(gg + 1) * 64, 256 + ch * 128 + gg * 64: 256 + ch * 128 + (gg + 1) * 64] = fw[2 * ch + gg]
        wsm[l, :, 512:1024] = f("sgu_w")[l].transpose(2, 0, 1).reshape(128, 512)
    gains = np.zeros((128, L, NG, 8), np.float32)
    names = ["ln_mix_pre", "ln_mix_post", "ln_xa_pre", "ln_mem", "ln_xa_post", "ln_ffn_pre", "ln_ffn_post"]
    for l in range(L):
        for gi, nm in enumerate(names):
            gains[:, l, gi, :] = f(nm)[l].reshape(8, 128).T
    sm = np.zeros((128, L, 6), np.float32)
    for l in range(L):
        sm[:, l, 0:2] = f("pool_scale")[l].reshape(2, 128).T
        sm[:, l, 2:6] = f("sgu_norm")[l].reshape(4, 128).T
    return dict(wbig=wbig.reshape(L * NBIG, 128, 4096), wgu=wgu.reshape(L * HC, 128, 2048),
                wdr=wdr.reshape(L * 8, 128, HC * 128), wsm=wsm,
                gains=gains.reshape(128, L * NG * 8), smalls=sm.reshape(128, L * 6),
                sgub=np.ascontiguousarray(f("sgu_b").reshape(L * 4, 128)))


def build_program(n_layers=L, dbg=None):
    nc = bass.Bass("TRN2", target_bir_lowering=False)
    dt = nc.dram_tensor
    xT_in = dt("xT", [S // T, 128, KC * T], F32, kind="ExternalInput")
    memT_in = dt("memT", [KC, 128, MEM], F32, kind="ExternalInput")
    wbig_in = dt("wbig", [L * NBIG, 128, 4096], F32, kind="ExternalInput")
    wgu_in = dt("wgu", [L * HC, 128, 2048], F32, kind="ExternalInput")
    wdr_in = dt("wdr", [L * 8, 128, HC * 128], F32, kind="ExternalInput")
    wsm_in = dt("wsm", [L, 128, 1024], F32, kind="ExternalInput")
    gains_in = dt("gains", [128, L * NG * 8], F32, kind="ExternalInput")
    smalls_in = dt("smalls", [128, L * 6], F32, kind="ExternalInput")
    sgub_in = dt("sgub", [L * 4, 128], F32, kind="ExternalInput")
    d64_in = dt("d64", [64, 128], BF16, kind="ExternalInput")
    e0_in = dt("e0", [128, 64 * 2 * 256], BF16, kind="ExternalInput")
    e1_in = dt("e1", [128, 64 * 2 * 128], BF16, kind="ExternalInput")
    cdft_in = dt("cdft", [128, 256], BF16, kind="ExternalInput")
    pb_in = dt("pb", [128, 60 * 128], BF16, kind="ExternalInput")
    outT = dt("outT", [KC, 128, OWN], F32, kind="ExternalOutput")
    wbig_b = dt("wbig_b", [L * NBIG, 128, 4096], BF16)
    wgu_b = dt("wgu_b", [L * HC, 128, 2048], BF16)
    wdr_b = dt("wdr_b", [L * 8, 128, HC * 128], BF16)
    wsm_b = dt("wsm_b", [L, 128, 1024], BF16)
    skind = "ExternalOutput" if dbg else "Internal"
    XT1 = dt("XT1", [S // T, 128, KC * T], F32, kind=skind)
    ZA = dt("ZA", [S, 256], BF16, kind=skind)
    ZC = dt("ZC", [2, S, 128], BF16, kind=skind)
    YC = dt("YC", [2, 128, S], BF16, kind=skind)

    DBA = dt("DBA", [128, 2 * 512], BF16, kind="ExternalOutput") if dbg else None
    DBB = dt("DBB", [128, 4 * 512], BF16, kind="ExternalOutput") if dbg else None
    DBU = dt("DBU", [128, 4 * 512], BF16, kind="ExternalOutput") if dbg else None

    P = Prog()
    es = ExitStack()
    with es:
        sb = lambda name, shape, dtype: es.enter_context(nc.sbuf_tensor("sb_" + name, shape, dtype))
        arena = sb("arena", [128, 58368], BF16)
        xTt = sb("xTt", [128, KC, T], F32)
        xTt2 = sb("xTt2", [128, KC, T], F32)
        yTt = sb("yTt", [128, KC, T], F32)
        uTt = sb("uTt", [128, 4, T], BF16)
        vsb = sb("vsb", [128, 2, 512], F32)
        tsb = sb("tsb", [128, 2, 512], F32)
        rstdA = sb("rstdA", [128, 512], F32)
        rstdB = sb("rstdB", [128, 512], F32)
        bsb4 = sb("bsb4", [128, 4, 128], F32)
        gains = sb("gains", [128, L * NG * 8], F32)
        smalls = sb("smalls", [128, L * 6], F32)
        stats = sb("stats", [128, 4, 8], F32)
        epst = sb("epst", [128, 2], F32)
        ssil = sb("ssil", [128, 2, 512], BF16)
        vn = sb("vn", [128, 2, 512], BF16)
        dsb = sb("dsb", [128, 2, 512], BF16)
        pbt = sb("pbt", [128, 12, 128], BF16)
        pbs = sb("pbs", [128, 12, 128], BF16)
        KTt = sb("KTt", [128, 8, MEM], BF16)
        Vt = sb("Vt", [128, 2, D], BF16)
        FTt = sb("FTt", [128, 2, 512], BF16)
        ycb = sb("ycb", [128, 2, 512], BF16)
        wsmt = sb("wsmt", [128, 1024], BF16)
        onesavg = sb("onesavg", [128, 128], BF16)
        ones1 = sb("ones1", [128, 128], BF16)
        banks = [es.enter_context(nc.psum_tensor(f"ps{i}", [128, 512], F32)) for i in range(8)]

        def av(off, n):
            return arena[:, off:off + n]
        hT = av(0, 4096).rearrange("p (k t) -> p k t", k=KC)
        sq = av(4096, 4096).rearrange("p (k t) -> p k t", k=KC)
        qT = av(8192, 4096).rearrange("p (k t) -> p k t", k=KC)
        oT = av(12288, 4096).rearrange("p (k t) -> p k t", k=KC)
        aT = av(16384, 11264).rearrange("p (k t) -> p k t", k=HC)
        ybT = av(27648, 2048).rearrange("p (k t) -> p k t", k=4)
        yaT = av(29696, 1024).rearrange("p (k t) -> p k t", k=2)
        Eat = av(30720, 2048).rearrange("p (q k t) -> p q k t", q=2, k=2)
        NBS = 3
        bigs = [av(32768 + i * 4096, 4096).rearrange("p (k c) -> p k c", k=KC) for i in range(NBS)]
        NGS = 3
        wgus = [av(45056 + i * 2048, 2048).rearrange("p (g k c) -> p g k c", g=2, k=KC) for i in range(NGS)]
        NDS = 2
        wds = [av(51200 + i * 2816, 2816).rearrange("p (k c) -> p k c", k=HC) for i in range(NDS)]
        zat = av(56832, 1536).rearrange("p (i c) -> p i c", i=6)
        d64 = arena[0:64, 57344:57472]
        cdft = av(57472, 256).rearrange("p (s t) -> p s t", s=2)
        zout = av(16384, 1024).rearrange("p (z c) -> p z c", z=2)
        Xh = av(0, 16384).rearrange("p (b c) -> p b c", b=128)
        Ah = av(16384, 16384)
        Ahv = Ah.rearrange("p (c m) -> p c m", m=128)
        NES = 2
        EKA = 8
        Eslots = [av(32768 + i * 4096, 4096) for i in range(NES)]
        GT = av(40960, 16384)
        memTt = av(0, 4096).bitcast(F32).rearrange("p (k m) -> p k m", k=KC)
        mnT = av(4096, 2048).rearrange("p (k m) -> p k m", k=KC)
        msq = av(6144, 2048).rearrange("p (k m) -> p k m", k=KC)

        bank_ctr = [0]

        held = set()

        def bank():
            while True:
                i = bank_ctr[0] % 8
                bank_ctr[0] += 1
                if i not in held:
                    return i

        def B(i):
            return banks[i]

        def G(l, gi, kc):
            c = (l * NG + gi) * 8 + kc
            return gains[:, c:c + 1]

        xbufs = [xTt, xTt2]
        C = dict(xi=0, hi=0)

        def X():
            return xbufs[C["xi"]]

        def XR(kc):
            return ("xT", C["xi"], kc)

        def XALL():
            return tuple(XR(k_) for k_ in range(KC))

        def H():
            return (hT, qT, qT)[C["hi"]]

        def SQ():
            return (sq, oT, sq)[C["hi"]]

        def HR(kc):
            return (("hT", kc), ("hT1", kc), ("qT", kc))[C["hi"]]

        def SR(kc):
            return (("sqk", kc), ("sqk1", kc), ("sqk", kc))[C["hi"]]

        def dma(eng, key, out, in_, reads, writes):
            return P.add(eng, lambda e, out=out, in_=in_: e.dma_start(out=out, in_=in_), reads, writes, dma=key)

        def mm(out, lhsT, rhs, start, stop, reads, writes):
            return P.add("pe", lambda e, o=out, l_=lhsT, r=rhs, s0=start, s1=stop:
                         e.matmul(o, l_, r, start=s0, stop=s1), reads, writes)

        def act(out, in_, func, reads, writes, scale=1.0, bias=0.0, accum=None):
            def fn(e, out=out, in_=in_, func=func, scale=scale, bias=bias, accum=accum):
                kw = {}
                if accum is not None:
                    kw["accum_out"] = accum
                return e.activation(out, in_, func, bias=bias, scale=scale, **kw)
            return P.add("act", fn, reads, writes)

        def vec(eng, fn, reads, writes):
            return P.add(eng, fn, reads, writes)

        dma("sp", "c0", gains[:, :], gains_in[:, :], (), ("gains",))
        dma("sp", "c1", smalls[:, :], smalls_in[:, :], (), ("smalls",))
        dma("sp", "c2", pbt[:, :, :], pb_in[:, 0:12 * 128].rearrange("p (s t) -> p s t", t=128), (), ("pbt",))
        vec("dve", lambda e: e.memset(onesavg[:, :], 1.0 / D), (), ("onesavg",))
        vec("dve", lambda e: e.memset(ones1[:, :], 1.0), (), ("ones1",))
        vec("dve", lambda e: e.memset(epst[:, 0:1], RMS_EPS), (), ("epst",))
        vec("dve", lambda e: e.memset(epst[:, 1:2], LN_EPS), ("epst",), ("epst",))
        GRP_A = (U_WK, U_WK + 1, U_WV, U_WV + 1)
        GRP_B = (U_U, U_V, U_WOUT, U_WOUT + 1, U_WQ, U_WQ + 1, U_WO, U_WO + 1)

        def cast_big(l, u, key):
            dma("pool", key, wbig_b[l * NBIG + u], wbig_in[l * NBIG + u], (), (("wb", l, u),))

        def cast_layer(l):
            cast_big(l, U_AZ, f"cast{l}z")
            dma("pool", f"cast{l}a", wsm_b[l], wsm_in[l], (), (("wsm", l),))
            for u in GRP_A:
                cast_big(l, u, f"cast{l}a")
            for u in GRP_B:
                cast_big(l, u, f"cast{l}b")
            for j in range(HC):
                dma("pool", f"cast{l}g", wgu_b[l * HC + j], wgu_in[l * HC + j], (), (("wgu", l, j),))
            for j in range(8):
                dma("pool", f"cast{l}d", wdr_b[l * 8 + j], wdr_in[l * 8 + j], (), (("wdr", l, j),))

        def big_deps(l, u):
            if u == U_AZ:
                return (("wb", l, U_AZ),)
            if u in GRP_A:
                return (("wsm", l),) + tuple(("wb", l, v) for v in GRP_A)
            return tuple(("wb", l, v) for v in GRP_B)
        cast_layer(0)

        class Ring:
            def __init__(self, name, slots, loader):
                self.name, self.slots, self.loader = name, slots, loader
                self.sched = []
                self.issued = 0
                self.used = 0
                self.fence = None

            def plan(self, items):
                self.sched.extend(items)

            def ensure(self, upto):
                lim = min(upto + 1, len(self.sched))
                if self.fence is not None:
                    lim = min(lim, self.fence)
                while self.issued < lim:
                    n = self.issued
                    s = n % len(self.slots)
                    self.loader(self.sched[n], self.slots[s], (self.name, s), f"{self.name}{s}")
                    self.issued += 1

            def use(self, item, ahead=None):
                n = self.used
                if self.sched[n] != item:
                    self.sched.insert(n, item)
                self.ensure(n + (len(self.slots) - 1 if ahead is None else ahead))
                self.used += 1
                s = n % len(self.slots)
                return self.slots[s], (self.name, s)

        def load_big(item, slot, res, key):
            l, u = item
            dma("sp", key, slot, wbig_b[l * NBIG + u].rearrange("p (k c) -> p k c", k=KC), big_deps(l, u), (res,))

        def load_gu(item, slot, res, key):
            l, j = item
            dma("sp", key, slot, wgu_b[l * HC + j].rearrange("p (g k c) -> p g k c", g=2, k=KC),
                tuple(("wgu", l, jj) for jj in range(HC)), (res,))

        def load_wd(item, slot, res, key):
            l, j = item
            dma("sp", key, slot, wdr_b[l * 8 + j].rearrange("p (k c) -> p k c", k=HC), tuple(("wdr", l, jj) for jj in range(8)), (res,))

        def ms_to_rstd(srcs, rstd_t, rstd_res, eps):
            n = srcs[0][0].shape[-1]
            bi = bank()
            for i, (ap, r) in enumerate(srcs):
                mm(B(bi)[:, 0:n], onesavg[:, :], ap, i == 0, i == len(srcs) - 1, (r, "onesavg"), (("ps", bi),))
            act(rstd_t[:, 0:n], B(bi)[:, 0:n], AF.Ln, (("ps", bi), "epst"), (rstd_res,), bias=epst[:, 0:1])
            act(rstd_t[:, 0:n], rstd_t[:, 0:n], AF.Exp, (rstd_res,), (rstd_res,), scale=-0.5)

        def prenorm_stats(rt=None, rres="rstdA"):
            rt = rstdA if rt is None else rt
            xa, sa = X(), SQ()
            for kc in range(KC):
                act(sa[:, kc, :], xa[:, kc, :], AF.Square, (XR(kc),), (SR(kc),))
            ms_to_rstd([(sa[:, kc, :], SR(kc)) for kc in range(KC)], rt, rres, RMS_EPS)

        def prenorm_apply(l, gi, rt=None, rres="rstdA"):
            rt = rstdA if rt is None else rt
            xa, ha = X(), H()
            for kc in range(KC):
                vec("dve", lambda e, kc=kc, xa=xa, ha=ha, rt=rt: e.scalar_tensor_tensor(
                    ha[:, kc, :], xa[:, kc, :], G(l, gi, kc), rt[:, :], ALU.mult, ALU.mult),
                    (XR(kc), rres, "gains"), (HR(kc),))

        def prenorm(l, gi, rt=None, rres="rstdA"):
            prenorm_stats(rt, rres)
            prenorm_apply(l, gi, rt, rres)

        def postnorm_stats():
            ms_to_rstd([(sq[:, kc, :], ("sqk", kc)) for kc in range(KC)], rstdB, "rstdB", RMS_EPS)

        def postnorm_apply(l, gi):
            xa = X()
            for kc in range(KC):
                vec("dve", lambda e, kc=kc: e.scalar_tensor_tensor(yTt[:, kc, :], yTt[:, kc, :], G(l, gi, kc),
                                                                 rstdB[:, :], ALU.mult, ALU.mult),
                    (("yT", kc), "rstdB", "gains"), (("yT", kc),))
                vec("dve", lambda e, kc=kc, xa=xa: e.tensor_tensor(xa[:, kc, :], xa[:, kc, :], yTt[:, kc, :], ALU.add),
                    (("yT", kc), XR(kc)), (XR(kc),))

        def postnorm_residual(l, gi):
            postnorm_stats()
            postnorm_apply(l, gi)

        def proj_to_y(ring, l, unit, rhs_list):
            for j in range(KC):
                if j % 4 == 0:
                    wap, wres = ring.use((l, unit + j // 4))
                bi = bank()
                for fc in range(KC):
                    rap, rres = rhs_list[fc]
                    mm(B(bi)[:, :], wap[:, fc, (j % 4) * 128:(j % 4 + 1) * 128], rap, fc == 0, fc == KC - 1,
                       (wres, rres), (("ps", bi),))
                act(yTt[:, j, :], B(bi)[:, :], AF.Copy, (("ps", bi),), (("yT", j),))
                act(sq[:, j, :], B(bi)[:, :], AF.Square, (("ps", bi),), (("sqk", j),))

        def kv_phase(l, bigring):
            dma("sp", "memT", memTt[:, :, :], memT_in[:, :, :].rearrange("k p m -> p k m"), (), ("memT",))
            act(msq[:, :, :], memTt[:, :, :], AF.Square, ("memT",), ("msq",))
            ms_to_rstd([(msq[:, kc, :], "msq") for kc in range(KC)], rstdA, "rstdA", RMS_EPS)
            for kc in range(KC):
                vec("dve", lambda e, kc=kc: e.scalar_tensor_tensor(mnT[:, kc, :], memTt[:, kc, :], G(l, G_MEM, kc),
                                                                 rstdA[:, 0:MEM], ALU.mult, ALU.mult),
                    ("memT", "rstdA", "gains"), ("mnT",))
            for hf in range(2):
                wap, wres = bigring.use((l, U_WK + hf))
                for jj in range(4):
                    j = hf * 4 + jj
                    bi = bank()
                    for kc in range(KC):
                        mm(B(bi)[:, 0:MEM], wap[:, kc, jj * 128:(jj + 1) * 128], mnT[:, kc, :], kc == 0, kc == KC - 1,
                           (wres, "mnT"), (("ps", bi),))
                    act(KTt[:, j, :], B(bi)[:, 0:MEM], AF.Copy, (("ps", bi),), ("KT",))
            for hf in range(2):
                wap, wres = bigring.use((l, U_WV + hf))
                for mc in range(2):
                    bi = bank()
                    for kc in range(KC):
                        mm(B(bi)[:, :], mnT[:, kc, mc * 128:(mc + 1) * 128], wap[:, kc, :], kc == 0, kc == KC - 1,
                           (wres, "mnT"), (("ps", bi),))
                    vec("dve", lambda e, bi=bi, mc=mc, hf=hf: e.tensor_copy(Vt[:, mc, hf * 512:(hf + 1) * 512], B(bi)[:, :]),
                        (("ps", bi),), ("V",))

        def load_x(l, blk):
            src = xT_in if l == 0 else XT1
            dma("sp", f"xld{C['xi']}", X()[:, :, :], src[blk].rearrange("p (k t) -> p k t", k=KC),
                (), XALL())

        def phase1_norm(l, blk):
            par = blk % 2
            C["xi"] = par
            C["hi"] = par
            prenorm(l, G_MIXPRE, rt=(rstdA, rstdB)[par], rres=("rstdA", "rstdB")[par])
            C["hi"] = 0

        def phase1_block(l, blk, azap, azres, nblk):
            par = blk % 2
            if blk == 0:
                C["xi"] = 0
                load_x(l, 0)
                if nblk > 1:
                    C["xi"] = 1
                    load_x(l, 1)
                phase1_norm(l, 0)
            if blk + 2 < nblk:
                C["xi"] = par
                load_x(l, blk + 2)
            if blk + 1 < nblk:
                phase1_norm(l, blk + 1)
            C["hi"] = par
            ha = H()
            for i in range(4):
                bi = bank()
                for kc in range(KC):
                    mm(B(bi)[:, :], ha[:, kc, i * 128:(i + 1) * 128], azap[:, kc, :], kc == 0, kc == KC - 1,
                       (HR(kc), azres), (("ps", bi),))
                zs = i % 2
                act(zout[:, zs, :], B(bi)[:, :], AF.Copy, (("ps", bi),), (("zout", zs),))
                t0 = blk * T + i * 128
                dma("sp", f"zst{zs}", ZA[t0:t0 + 128, :], zout[:, zs, 0:256], (("zout", zs),), ())
                for h in range(2):
                    dma("sp", f"zst{zs}", ZC[h, t0:t0 + 128, :], zout[:, zs, 256 + h * 128:256 + (h + 1) * 128],
                        (("zout", zs),), ())
            C["hi"] = 0

        def fft_phase(l, full):
            nkb = 128 if full else 64
            ntok = nkb * 64
            ew = 2 * nkb
            e_in = e0_in if full else e1_in
            dma("sp", "wsm", wsmt[:, :], wsm_b[l], big_deps(l, U_WK), ("wsmt",))
            dma("sp", "c3", cdft[:, :, :], cdft_in[:, :].rearrange("p (s t) -> p s t", t=128), (), ("cdft",))
            dma("sp", "c4", d64, d64_in[:, :], (), ("d64",))
            fnetw = wsmt[:, 256:512].rearrange("p (c d) -> p c d", c=2)
            for h in range(2):
                dma("sp", "xh", Xh[0:64, :, :], ZC[h].rearrange("(a b) c -> a b c", b=128), (), ("Xh",))
                for c0 in range(0, 128, 4):
                    bi = bank()
                    for cc in range(4):
                        mm(B(bi)[:, cc * 128:(cc + 1) * 128], Xh[0:64, :, c0 + cc], d64, True, True,
                           ("Xh", "d64"), (("ps", bi),))
                    src = B(bi)[:, :].rearrange("p (c m) -> p c m", c=4)
                    dst = Ahv[:, c0:c0 + 4, :]
                    if (c0 // 4) % 2 == 0:
                        vec("dve", lambda e, dst=dst, src=src: e.tensor_copy(dst, src), (("ps", bi),), ("Ah",))
                    else:
                        act(dst, src, AF.Copy, (("ps", bi),), ("Ah",))
                per_bank = 512 // ew
                for kg in range(64 // EKA):
                    es_i = kg % NES
                    eslot = Eslots[es_i][:, 0:EKA * 2 * ew].rearrange("p (a w n) -> p a w n", a=EKA, w=2)
                    dma("sp", f"els{es_i}", eslot,
                        e_in[:, kg * EKA * 2 * ew:(kg + 1) * EKA * 2 * ew].rearrange("p (a w n) -> p a w n", a=EKA, w=2),
                        (), (("E", es_i),))
                    for k0 in range(0, EKA, per_bank):
                        bi = bank()
                        for kk in range(per_bank):
                            ka = kg * EKA + k0 + kk
                            osl = B(bi)[:, kk * ew:(kk + 1) * ew]
                            mm(osl, Ahv[:, :, ka], eslot[:, k0 + kk, 0, :], True, False, ("Ah", ("E", es_i)), (("ps", bi),))
                            mm(osl, Ahv[:, :, 64 + ka], eslot[:, k0 + kk, 1, :], False, True, ("Ah", ("E", es_i)), (("ps", bi),))
                        for kk in range(per_bank):
                            ka = kg * EKA + k0 + kk
                            src = B(bi)[:, kk * ew:(kk + 1) * ew].rearrange("p (r b) -> p r b", r=2)
                            dst = GT[:, 0:2 * ntok].rearrange("p (r a b) -> p r a b", r=2, a=64)[:, :, ka, :]
                            if kk % 2 == 0:
                                vec("dve", lambda e, dst=dst, src=src: e.tensor_copy(dst, src), (("ps", bi),), ("GT",))
                            else:
                                act(dst, src, AF.Copy, (("ps", bi),), ("GT",))
                GTp = GT[:, 0:2 * ntok].rearrange("p (r a b) -> p r a b", r=2, a=64)
                for cb in range(ntok // 512):
                    bi = bank()
                    mm(B(bi)[:, :], cdft[:, 0, :], GTp[:, 0, :, cb * 8:(cb + 1) * 8].rearrange("p a b -> p b a"), True, False,
                       ("GT", "cdft"), (("ps", bi),))
                    mm(B(bi)[:, :], cdft[:, 1, :], GTp[:, 1, :, cb * 8:(cb + 1) * 8].rearrange("p a b -> p b a"), False, True,
                       ("GT", "cdft"), (("ps", bi),))
                    fs = cb % 2
                    act(FTt[:, fs, :], B(bi)[:, :], AF.Copy, (("ps", bi),), (("FT", fs),), scale=FFT_SCALE)
                    b2 = bank()
                    mm(B(b2)[:, :], fnetw[:, h, :], FTt[:, fs, :], True, True, (("FT", fs), "wsmt"), (("ps", b2),))
                    vec("dve", lambda e, b2=b2, fs=fs: e.tensor_copy(ycb[:, fs, :], B(b2)[:, :]), (("ps", b2),), (("ycs", fs),))
                    dma("sp", f"ycst{fs}", YC[h, :, cb * 512:(cb + 1) * 512], ycb[:, fs, :], (("ycs", fs),), ())

        def dbg_dump(blk):
            if not dbg:
                return
            if blk == 0:
                dma("sp", "dbd", DBA[:, :].rearrange("p (k t) -> p k t", k=2), yaT, tuple(("yaT", c_) for c_ in range(2)), ())
                dma("sp", "dbd", DBB[:, :].rearrange("p (k t) -> p k t", k=4), ybT, tuple(("ybT", c_) for c_ in range(4)), ())
                dma("sp", "dbd", DBU[:, :].rearrange("p (k t) -> p k t", k=4), uTt[:, :, :], tuple(("uT", c_) for c_ in range(4)), ())

        def phase2_loads(l, blk):
            C["xi"] = blk % 2
            C["hi"] = 0
            load_x(l, blk)
            dma("sp", "ycld", ycb[:, :, :], YC[:, :, blk * T:(blk + 1) * T].rearrange("h p t -> p h t"),
                (), ("ycb",))

        def phase2_loads_za(l, blk):
            t0 = blk * 4
            tp, tn = (t0 - 1) % 64, (t0 + 4) % 64
            dma("sp", "zald0", zat[:, 0, :], ZA[tp * 128:(tp + 1) * 128, :], (), (("zat", 0),))
            dma("sp", "zald1", zat[:, 1:5, :], ZA[t0 * 128:(t0 + 4) * 128, :].rearrange("(i p) c -> p i c", p=128),
                (), (("zat", 1),))
            dma("sp", "zald2", zat[:, 5, :], ZA[tn * 128:(tn + 1) * 128, :], (), (("zat", 2),))
            spec = [(i, {0: 1, 31: 2, 32: 3, 63: 4}[t0 + i]) for i in range(4) if (t0 + i) in (0, 31, 32, 63)]
            if spec:
                pset_ = spec[0][1]
                dma("sp", "pbs", pbs[:, :, :], pb_in[:, pset_ * 12 * 128:(pset_ + 1) * 12 * 128].rearrange("p (s t) -> p s t", t=128),
                    (), ("pbs",))

        def phase2_head(l, blk):
            xi_save = C["xi"]
            C["xi"] = blk % 2
            C["hi"] = 2
            prenorm(l, G_MIXPRE)
            C["hi"] = 0
            C["xi"] = xi_save

        def make_fe(l, blk, bigring):
            poolw = wsmt[:, 0:256].rearrange("p (c d) -> p c d", c=2)
            sguw = wsmt[:, 512:1024].rearrange("p (h q) -> p h q", h=4)
            t0 = blk * 4
            st_ = {}

            def zv_tile(i):
                bi = bank()
                for kc in range(KC):
                    mm(B(bi)[:, :], qT[:, kc, i * 128:(i + 1) * 128], st_["vwap"][:, kc, :], kc == 0, kc == KC - 1,
                       (("qT", kc), st_["vwres"]), (("ps", bi),))
                st = stats[:, i, :]
                vs = i % 2
                act(vsb[:, vs, :], B(bi)[:, :], AF.Copy, (("ps", bi),), (("vsb", vs),))
                act(FTt[:, vs, :], B(bi)[:, :], AF.Square, (("ps", bi),), (("vsq", vs),))
                vec("dve", lambda e, st=st, vs=vs: e.reduce_sum(st[:, 0:1], vsb[:, vs, :], mybir.AxisListType.X),
                    (("vsb", vs),), (("st", i),))
                vec("dve", lambda e, st=st, vs=vs: e.reduce_sum(st[:, 1:2], FTt[:, vs, :], mybir.AxisListType.X),
                    (("vsq", vs), ("st", i)), (("st", i),))
                vec("dve", lambda e, st=st: e.tensor_scalar(st[:, 2:3], st[:, 0:1], 1.0 / 512, None, ALU.mult),
                    (("st", i),), (("st", i),))
                vec("dve", lambda e, st=st: e.tensor_tensor(st[:, 3:4], st[:, 2:3], st[:, 2:3], ALU.mult),
                    (("st", i),), (("st", i),))
                vec("dve", lambda e, st=st: e.scalar_tensor_tensor(st[:, 4:5], st[:, 1:2], 1.0 / 512, st[:, 3:4],
                                                                 ALU.mult, ALU.subtract), (("st", i),), (("st", i),))
                act(st[:, 5:6], st[:, 4:5], AF.Ln, (("st", i), "epst"), (("st", i),), bias=epst[:, 1:2])
                act(st[:, 5:6], st[:, 5:6], AF.Exp, (("st", i),), (("st", i),), scale=-0.5)
                vec("dve", lambda e, st=st, vs=vs: e.tensor_scalar(vn[:, vs, :], vsb[:, vs, :], st[:, 2:3], st[:, 5:6],
                                                                 ALU.subtract, ALU.mult),
                    (("vsb", vs), ("st", i)), (("vn", vs),))

            def sgu_tile(i):
                vs = i % 2
                for h in range(4):
                    mm(B(st_["sgb"][h])[:, i * 128:(i + 1) * 128], vn[:, vs, h * 128:(h + 1) * 128], sguw[:, h, :], True, True,
                       (("vn", vs), "wsmt"), (("ps", st_["sgb"][h]),))

            def pool_chunk(ch):
                pbk = [bank(), bank()]
                for gg in range(2):
                    g = 2 * ch + gg
                    for i in range(4):
                        ti = t0 + i
                        ptile = pbs if ti in (0, 31, 32, 63) else pbt
                        for r in range(3):
                            mm(B(pbk[gg])[:, i * 128:(i + 1) * 128], zat[:, i + r, ch * 128:(ch + 1) * 128],
                               ptile[:, r * 4 + g, :], r == 0, r == 2,
                               (("zat", 0), ("zat", 1), ("zat", 2), "pbt", "pbs"), (("ps", pbk[gg]),))
                vec("dve", lambda e, ch=ch, b0=pbk[0]: e.tensor_copy(dsb[0:64, ch, :], B(b0)[0:64, :]),
                    (("ps", pbk[0]),), (("dsb", ch, 0),))
                act(dsb[64:128, ch, :], B(pbk[1])[64:128, :], AF.Copy, (("ps", pbk[1]),), (("dsb", ch, 1),))

            def pool_out(ch):
                bi = bank()
                mm(B(bi)[:, :], poolw[:, ch, :], dsb[:, ch, :], True, True, (("dsb", ch, 0), ("dsb", ch, 1), "wsmt"), (("ps", bi),))
                vec("dve", lambda e, bi=bi, ch=ch: e.tensor_scalar(yaT[:, ch, :], B(bi)[:, :],
                                                                    smalls[:, l * 6 + ch: l * 6 + ch + 1], None, ALU.mult),
                    (("ps", bi), "smalls"), (("yaT", ch),))

            def part1():
                st_["vwap"], st_["vwres"] = bigring.use((l, U_V))
                st_["sgb"] = [bank() for _ in range(4)]
                held.update(st_["sgb"])
                zv_tile(0)
                zv_tile(1)
                uwap, uwres = bigring.use((l, U_U), ahead=1)
                for j in range(4):
                    bi = bank()
                    for kc in range(KC):
                        mm(B(bi)[:, :], uwap[:, kc, j * 128:(j + 1) * 128], qT[:, kc, :], kc == 0, kc == KC - 1,
                           (uwres, ("qT", kc)), (("ps", bi),))
                    act(uTt[:, j, :], B(bi)[:, :], AF.Copy, (("ps", bi),), (("uT", j),))

            def part2():
                part2_body()

            def part2_body():
              sgb = st_["sgb"]
              if True:
                sgu_tile(0)
                zv_tile(2)
                sgu_tile(1)
                zv_tile(3)
                sgu_tile(2)
                pool_out(0)
                pool_out(1)
                sgu_tile(3)
                for h in range(4):
                    ts_ = h % 2
                    sc = smalls[:, l * 6 + 2 + h: l * 6 + 3 + h]
                    bsl = bsb4[:, h, :]
                    bb = bass.AP(tensor=bsl.tensor, offset=bsl.offset, ap=[list(bsl.ap[0]), [0, 4], [1, 128]])
                    vec("dve", lambda e, h=h, ts_=ts_, sc=sc, bb=bb: e.scalar_tensor_tensor(
                        tsb[:, ts_, :].rearrange("p (i q) -> p i q", i=4), B(sgb[h])[:, :].rearrange("p (i q) -> p i q", i=4), sc, bb,
                        ALU.mult, ALU.add), (("ps", sgb[h]), "smalls") + tuple(("bsb4", h_) for h_ in range(4)), (("tsb", ts_),))
                    vec("pool", lambda e, h=h, ts_=ts_: e.tensor_tensor(ybT[:, h, :], tsb[:, ts_, :], uTt[:, h, :], ALU.mult),
                        (("tsb", ts_), ("uT", h)), (("ybT", h),))
                held.difference_update(sgb)

            def part0():
                pool_chunk(0)
                pool_chunk(1)

            return part0, part1, part2

        def phase2_body1(l, blk, bigring, guring, nb2):
            C["xi"] = blk % 2
            C["hi"] = 0
            cat = [(yaT[:, 0, :], ("yaT", 0)), (yaT[:, 1, :], ("yaT", 1))] + \
                  [(ybT[:, h, :], ("ybT", h)) for h in range(4)] + \
                  [(ycb[:, 0, :], "ycb"), (ycb[:, 1, :], "ycb")]
            nxt = blk + 1 < nb2
            if nxt:
                phase2_loads_za(l, blk + 1)
                fe0, fe1, fe2 = make_fe(l, blk + 1, bigring)
            proj_to_y(bigring, l, U_WOUT, cat)
            postnorm_residual(l, G_MIXPOST)
            if nxt:
                fe0()
            if blk + 1 < nb2:
                phase2_loads(l, blk + 1)
                C["xi"] = blk % 2
            prenorm(l, G_XAPRE)
            for hf in range(2):
                wap, wres = bigring.use((l, U_WQ + hf))
                for jj in range(4):
                    j = hf * 4 + jj
                    bi = bank()
                    for kc in range(KC):
                        mm(B(bi)[:, :], wap[:, kc, jj * 128:(jj + 1) * 128], hT[:, kc, :], kc == 0, kc == KC - 1,
                           (wres, ("hT", kc)), (("ps", bi),))
                    act(qT[:, j, :], B(bi)[:, :], AF.Copy, (("ps", bi),), (("qT", j),), scale=1.0 / 16.0)
            for h in range(4):
                for mc in range(2):
                    bi = bank()
                    for dc in range(2):
                        mm(B(bi)[:, :], KTt[:, 2 * h + dc, mc * 128:(mc + 1) * 128], qT[:, 2 * h + dc, :], dc == 0, dc == 1,
                           ("KT", ("qT", 2 * h + dc)), (("ps", bi),))
                    act(Eat[:, h % 2, mc, :], B(bi)[:, :], AF.Exp, (("ps", bi),), (("Eat", h % 2, mc),))
                bd = bank()
                for mc in range(2):
                    mm(B(bd)[:, :], ones1[:, :], Eat[:, h % 2, mc, :], mc == 0, mc == 1, (("Eat", h % 2, mc), "ones1"), (("ps", bd),))
                act(tsb[:, h % 2, :], B(bd)[:, :], AF.Ln, (("ps", bd),), (("tsb", h % 2),))
                act(tsb[:, h % 2, :], tsb[:, h % 2, :], AF.Exp, (("tsb", h % 2),), (("tsb", h % 2),), scale=-1.0)
                for dc in range(2):
                    bi = bank()
                    for mc in range(2):
                        mm(B(bi)[:, :], Vt[:, mc, (2 * h + dc) * 128:(2 * h + dc + 1) * 128], Eat[:, h % 2, mc, :], mc == 0, mc == 1,
                           ("V", ("Eat", h % 2, mc)), (("ps", bi),))
                    vec("dve", lambda e, bi=bi, h=h, dc=dc: e.tensor_tensor(oT[:, 2 * h + dc, :], B(bi)[:, :], tsb[:, h % 2, :], ALU.mult),
                        (("ps", bi), ("tsb", h % 2)), (("oT", 2 * h + dc),))
            if nxt:
                phase2_head(l, blk + 1)
            proj_to_y(bigring, l, U_WO, [(oT[:, j, :], ("oT", j)) for j in range(KC)])
            postnorm_stats()
            if nxt:
                fe1()
            postnorm_apply(l, G_XAPOST)
            prenorm_stats()
            prenorm_apply(l, G_FFNPRE)
            if nxt:
                fe2()
            for j in range(HC):
                wap, wres = guring.use((l, j))
                bg, bu = bank(), bank()
                for kc in range(KC):
                    mm(B(bg)[:, :], wap[:, 0, kc, :], hT[:, kc, :], kc == 0, kc == KC - 1, (wres, ("hT", kc)), (("ps", bg),))
                for kc in range(KC):
                    mm(B(bu)[:, :], wap[:, 1, kc, :], hT[:, kc, :], kc == 0, kc == KC - 1, (wres, ("hT", kc)), (("ps", bu),))
                ss = j % 2
                act(ssil[:, ss, :], B(bg)[:, :], AF.Silu, (("ps", bg),), (("ssil", ss),))
                vec("dve", lambda e, bu=bu, ss=ss, j=j: e.tensor_tensor(aT[:, j, :], B(bu)[:, :], ssil[:, ss, :], ALU.mult),
                    (("ps", bu), ("ssil", ss)), (("aT", j),))

        def phase2_body2(l, blk, wdring, last):
            C["xi"] = blk % 2
            C["hi"] = 0
            bigring.ensure(bigring.used + 1)
            for j in range(KC):
                wap, wres = wdring.use((l, j))
                bi = bank()
                for hc in range(HC):
                    mm(B(bi)[:, :], wap[:, hc, :], aT[:, hc, :], hc == 0, hc == HC - 1, (wres, ("aT", hc)), (("ps", bi),))
                act(yTt[:, j, :], B(bi)[:, :], AF.Copy, (("ps", bi),), (("yT", j),))
                act(sq[:, j, :], B(bi)[:, :], AF.Square, (("ps", bi),), (("sqk", j),))
            postnorm_residual(l, G_FFNPOST)
            if last:
                dma("sp", f"xst{C['xi']}", outT[:, :, blk * T:(blk + 1) * T].rearrange("k p t -> p k t"), X()[:, :, :],
                    XALL(), ())
            else:
                dma("sp", f"xst{C['xi']}", XT1[blk].rearrange("p (k t) -> p k t", k=KC), X()[:, :, :], XALL(), ())

        bigring = Ring("big", [(b_, None) for b_ in bigs], None)
        guring = Ring("gu", [(g_, None) for g_ in wgus], None)
        wdring = Ring("wd", [(w_, None) for w_ in wds], None)
        bigring.slots = bigs
        bigring.loader = load_big
        guring.slots = wgus
        guring.loader = load_gu
        wdring.slots = wds
        wdring.loader = load_wd

        nblk_all = S // T
        nblk_own = OWN // T
        p2start, guend, wdend = {}, {}, {}
        for l in range(n_layers):
            nb2 = nblk_all if l < L - 1 else nblk_own
            bigring.plan([(l, U_AZ)])
            bigring.plan([(l, U_WK), (l, U_WK + 1), (l, U_WV), (l, U_WV + 1)])
            p2start[l] = len(bigring.sched)
            bigring.plan([(l, U_V), (l, U_U)])
            for blk in range(nb2):
                bigring.plan([(l, U_WOUT), (l, U_WOUT + 1), (l, U_WQ), (l, U_WQ + 1), (l, U_WO), (l, U_WO + 1)])
                if blk + 1 < nb2:
                    bigring.plan([(l, U_V), (l, U_U)])
                guring.plan([(l, j) for j in range(HC)])
                wdring.plan([(l, j) for j in range(KC)])
            guend[l] = len(guring.sched)
            wdend[l] = len(wdring.sched)

        def nop_fn(e):
            return e.nop()

        stages = ["casts", "p1", "kv", "fft", "p2"]
        dstage = stages.index(dbg.split(":")[0]) if dbg else 99
        dnblk = int(dbg.split(":")[1]) if (dbg and ":" in dbg) else None
        for l in range(n_layers):
            full = l < L - 1
            nb2 = nblk_all if full else nblk_own
            if dbg and (l > 0 or dstage < 1):
                break
            bigring.fence = p2start[l]
            guring.fence = guend[l - 1] if l > 0 else 0
            wdring.fence = wdend[l - 1] if l > 0 else 0
            azap, azres = bigring.use((l, U_AZ))
            for blk in range(nblk_all):
                phase1_block(l, blk, azap, azres, nblk_all)
            P.barrier("sp", nop_fn)
            if dstage < 2:
                break
            kv_phase(l, bigring)
            P.barrier("sp", nop_fn)
            if dstage < 3:
                break
            fft_phase(l, full)
            P.barrier("sp", nop_fn)
            if l + 1 < n_layers and not dbg:
                cast_layer(l + 1)
            if dstage < 4:
                break
            bigring.fence = p2start.get(l + 1)
            guring.fence = guend[l]
            wdring.fence = wdend[l]
            if dnblk is not None:
                nb2 = dnblk
            for h in range(4):
                dma("sp", "bsb", bsb4[:, h, :], sgub_in[l * 4 + h:l * 4 + h + 1, :].partition_broadcast(128),
                    (), (("bsb4", h),))
            phase2_loads(l, 0)
            phase2_loads_za(l, 0)
            C["xi"] = 0
            phase2_head(l, 0)
            fe0, fe1, fe2 = make_fe(l, 0, bigring)
            fe1()
            fe0()
            fe2()
            for blk in range(nb2):
                phase2_body1(l, blk, bigring, guring, nb2)
                phase2_body2(l, blk, wdring, last=(l == L - 1))
            P.barrier("sp", nop_fn)

        P.barrier("sp", nop_fn)
        keys = P.finalize()
        sems = {}
        for k in keys:
            sems[k] = es.enter_context(nc.semaphore(("s_" + str(k[1])).replace(" ", "")[:24]))
        with nc.Block() as block:
            @block.tensor
            def _(e):
                P.emit_engine("pe", e, sems)

            @block.scalar
            def _(e):
                P.emit_engine("act", e, sems)

            @block.vector
            def _(e):
                P.emit_engine("dve", e, sems)

            @block.gpsimd
            def _(e):
                P.emit_engine("pool", e, sems)

            @block.sync
            def _(e):
                P.emit_engine("sp", e, sems)
    return nc


_CACHE = {}


def kernel(**inputs):
    x = np.asarray(inputs["x"], dtype=np.float32)
    mem = np.asarray(inputs["mem"], dtype=np.float32)
    wts = _prep_weights(inputs)
    consts = [_const_tables(0), _const_tables(1)]
    in_maps = []
    for c in range(8):
        b, half = c // 2, c % 2
        xl = np.roll(x[b], -OWN * half, axis=0)
        m = dict(wts)
        m.update(consts[half])
        m["xT"] = np.ascontiguousarray(xl.T.reshape(KC, 128, S // T, T).transpose(2, 1, 0, 3)).reshape(S // T, 128, KC * T)
        m["memT"] = np.ascontiguousarray(mem[b].T).reshape(KC, 128, MEM)
        in_maps.append(m)
    if "nc" not in _CACHE:
        _CACHE["nc"] = build_program()
    res = run_bass_kernel_spmd(_CACHE["nc"], in_maps, core_ids=list(range(8)))
    out = np.empty((4, S, D), np.float32)
    for c in range(8):
        b, half = c // 2, c % 2
        oT = np.asarray(res.results[c]["outT"]).reshape(D, OWN)
        out[b, half * OWN:(half + 1) * OWN, :] = oT.T
    return out
```

```python
import os
from contextlib import ExitStack
import numpy as np
import ml_dtypes
import concourse.bass as bass
import concourse.mybir as mybir
from concourse.bass_utils import run_bass_kernel_spmd

F32 = mybir.dt.float32
BF16 = mybir.dt.bfloat16
ALU = mybir.AluOpType
AF = mybir.ActivationFunctionType

D = 1024
S = 8192
OWN = 4096
L = 2
T = 512
KC = 8
FF = 2816
HC = 22
MEM = 256
RMS_EPS = 1e-6
LN_EPS = 1e-5
NG = 7
G_MIXPRE, G_MIXPOST, G_XAPRE, G_MEM, G_XAPOST, G_FFNPRE, G_FFNPOST = range(7)
NBIG = 13
U_AZ, U_U, U_V, U_WOUT, U_WQ, U_WK, U_WV, U_WO = 0, 1, 2, 3, 5, 7, 9, 11
FFT_SCALE = float(1.0 / np.sqrt(S * 64.0))
DBG_CUT = [99]


class Prog:
    def __init__(self):
        self.ops = []
        self.res = {}
        self.floor = None
        self.last_eng = {}
        self.last_dma = {}

    def add(self, eng, fn, reads=(), writes=(), dma=None):
        oid = len(self.ops)
        deps = set()
        if self.floor is not None:
            deps.add(self.floor)
        for r in reads:
            st = self.res.get(r)
            if st is not None and st[0] is not None:
                deps.add(st[0])
        for w in writes:
            st = self.res.get(w)
            if st is not None:
                if st[0] is not None:
                    deps.add(st[0])
                deps.update(st[1])
        for r in reads:
            self.res.setdefault(r, [None, []])[1].append(oid)
        for w in writes:
            self.res[w] = [oid, []]
        deps.discard(oid)
        self.ops.append(dict(eng=eng, fn=fn, deps=deps, dma=dma, marked=False, waits=[]))
        if dma is None:
            self.last_eng[eng] = oid
        else:
            self.last_dma[dma] = oid
        return oid

    def barrier(self, eng, fn):
        deps = set(self.last_eng.values()) | set(v for k, v in self.last_dma.items() if not k.startswith("cast"))
        oid = len(self.ops)
        if self.floor is not None:
            deps.add(self.floor)
        self.ops.append(dict(eng=eng, fn=fn, deps=deps, dma=None, marked=False, waits=[]))
        self.last_eng[eng] = oid
        self.floor = oid
        self.res = {k: v for k, v in self.res.items() if isinstance(k, tuple) and k[0] in ("wb", "wgu", "wdr", "wsm")}
        return oid

    def finalize(self):
        ops = self.ops
        for op in ops:
            for d in sorted(op["deps"]):
                a = ops[d]
                if a["eng"] == "pe" and op["eng"] == "pe" and a["dma"] is None and op["dma"] is None:
                    continue
                a["marked"] = True
                op["waits"].append(d)
        cnt = {}
        for op in ops:
            if op["dma"] is not None:
                op["marked"] = True
            if op["marked"]:
                key = ("dma", op["dma"]) if op["dma"] is not None else ("eng", op["eng"])
                inc = 16 if op["dma"] is not None else 1
                cnt[key] = cnt.get(key, 0) + inc
                op["sem"] = key
                op["val"] = cnt[key]
                op["inc"] = inc
        return sorted(cnt.keys(), key=str)

    def emit_engine(self, eng_name, eng, sems):
        waited = {}
        ops = self.ops
        for op in ops:
            if op["eng"] != eng_name:
                continue
            need = {}
            for d in op["waits"]:
                a = ops[d]
                if a["val"] > need.get(a["sem"], 0):
                    need[a["sem"]] = a["val"]
            for k in sorted(need, key=str):
                if waited.get(k, 0) >= need[k]:
                    continue
                eng.wait_ge(sems[k], need[k])
                waited[k] = need[k]
            ins = op["fn"](eng)
            if op["marked"]:
                ins.then_inc(sems[op["sem"]], op["inc"])
                if op["sem"] == ("eng", eng_name):
                    pass


def _bf(a):
    return np.ascontiguousarray(a.astype(ml_dtypes.bfloat16))


def _const_tables(half):
    off = OWN * half
    a = np.arange(64)[:, None].astype(np.float64)
    ka = np.arange(64)[None, :].astype(np.float64)
    ang = 2 * np.pi * a * ka / 64.0
    d64 = np.concatenate([np.cos(ang), -np.sin(ang)], axis=1)
    b = np.arange(128, dtype=np.int64)[:, None, None]
    kaa = np.arange(64, dtype=np.int64)[None, :, None]

    def etab(nkb):
        kb = np.arange(nkb, dtype=np.int64)[None, None, :]
        kloc = kaa + 64 * kb
        k = (kloc + off) % S
        sig = np.where((k * half) % 2 == 1, -1.0, 1.0)
        th = 2 * np.pi * ((k * b) % S).astype(np.float64) / S
        ec = sig * np.cos(th)
        es = sig * np.sin(th)
        e = np.empty((128, 64, 2, 2 * nkb), np.float64)
        e[:, :, 0, :nkb] = ec
        e[:, :, 0, nkb:] = -es
        e[:, :, 1, :nkb] = es
        e[:, :, 1, nkb:] = ec
        return e
    e0 = etab(128)
    e1 = etab(64)
    c = np.arange(64)[:, None].astype(np.float64)
    j = np.arange(64)[None, :].astype(np.float64)
    ph = 2 * np.pi * c * j / 64.0
    cd = np.zeros((128, 2, 128), np.float64)
    for hh in range(2):
        cd[hh * 64:(hh + 1) * 64, 0, hh * 64:(hh + 1) * 64] = np.cos(ph)
        cd[hh * 64:(hh + 1) * 64, 1, hh * 64:(hh + 1) * 64] = np.sin(ph)
    kinds = np.zeros((3, 3, 4, 128, 128), np.float64)
    for kind in range(3):
        for g, w in enumerate((2, 4, 8, 16)):
            left = w // 2
            right = w - 1 - left
            for tt in range(128):
                lo = tt - left
                hi = tt + right
                if kind == 1:
                    lo = max(lo, 0)
                if kind == 2:
                    hi = min(hi, 127)
                cntv = hi - lo + 1
                for sr in range(lo, hi + 1):
                    r = 0 if sr < 0 else (1 if sr < 128 else 2)
                    kinds[kind, r, g, sr % 128, tt] += 1.0 / cntv
                kinds[kind, 1, g, tt, tt] -= 1.0
    if half == 0:
        sel = [0, 1, 0, 0, 2]
    else:
        sel = [0, 0, 2, 1, 0]
    pb = np.stack([kinds[s_] for s_ in sel], axis=0)
    pb = pb.transpose(3, 0, 1, 2, 4).reshape(128, 60 * 128)
    return dict(d64=_bf(d64), e0=_bf(e0.reshape(128, -1)), e1=_bf(e1.reshape(128, -1)),
                cdft=_bf(cd.reshape(128, 256)), pb=_bf(pb))


def _big_unit(w, c0):
    return w[:, c0:c0 + 512].reshape(8, 128, 512).transpose(1, 0, 2).reshape(128, 4096)


def _prep_weights(inp):
    f = lambda k: np.asarray(inp[k], dtype=np.float32)
    w_in, w_out = f("w_in"), f("w_out")
    wq, wk, wv, wo = f("xa_wq"), f("xa_wk"), f("xa_wv"), f("xa_wo")
    wg, wu, wd = f("ffn_wg"), f("ffn_wu"), f("ffn_wd")
    wbig = np.zeros((L, NBIG, 128, 4096), np.float32)
    wgu = np.zeros((L, HC, 128, 2, 8, 128), np.float32)
    wdr = np.zeros((L, 8, 128, HC, 128), np.float32)
    wsm = np.zeros((L, 128, 1024), np.float32)
    for l in range(L):
        az = np.concatenate([w_in[l][:, 0:256], w_in[l][:, 1280:1536]], axis=1)
        wbig[l, U_AZ] = _big_unit(az, 0)
        wbig[l, U_U] = _big_unit(w_in[l], 256)
        wbig[l, U_V] = _big_unit(w_in[l], 768)
        for hf in range(2):
            wbig[l, U_WOUT + hf] = _big_unit(w_out[l], 512 * hf)
            wbig[l, U_WQ + hf] = _big_unit(wq[l], 512 * hf)
            wbig[l, U_WK + hf] = _big_unit(wk[l], 512 * hf)
            wbig[l, U_WV + hf] = _big_unit(wv[l], 512 * hf)
            wbig[l, U_WO + hf] = _big_unit(wo[l], 512 * hf)
        wgu[l, :, :, 0] = wg[l].reshape(8, 128, HC, 128).transpose(2, 1, 0, 3)
        wgu[l, :, :, 1] = wu[l].reshape(8, 128, HC, 128).transpose(2, 1, 0, 3)
        wdr[l] = wd[l].reshape(HC, 128, 8, 128).transpose(2, 1, 0, 3)
        pw, fw = f("pool_w")[l], f("fnet_w")[l]
        for ch in range(2):
            for gg in range(2):
                wsm[l, gg * 64:(gg + 1) * 64, ch * 128 + gg * 64: ch * 128 + (gg + 1) * 64] = pw[2 * ch + gg]
                wsm[l, gg * 64:(gg + 1) * 64, 256 + ch * 128 + gg * 64: 256 + ch * 128 + (gg + 1) * 64] = fw[2 * ch + gg]
        wsm[l, :, 512:1024] = f("sgu_w")[l].transpose(2, 0, 1).reshape(128, 512)
    gains = np.zeros((128, L, NG, 8), np.float32)
    names = ["ln_mix_pre", "ln_mix_post", "ln_xa_pre", "ln_mem", "ln_xa_post", "ln_ffn_pre", "ln_ffn_post"]
    for l in range(L):
        for gi, nm in enumerate(names):
            gains[:, l, gi, :] = f(nm)[l].reshape(8, 128).T
    sm = np.zeros((128, L, 6), np.float32)
    for l in range(L):
        sm[:, l, 0:2] = f("pool_scale")[l].reshape(2, 128).T
        sm[:, l, 2:6] = f("sgu_norm")[l].reshape(4, 128).T
    return dict(wbig=wbig.reshape(L * NBIG, 128, 4096), wgu=wgu.reshape(L * HC, 128, 2048),
                wdr=wdr.reshape(L * 8, 128, HC * 128), wsm=wsm,
                gains=gains.reshape(128, L * NG * 8), smalls=sm.reshape(128, L * 6),
                sgub=np.ascontiguousarray(f("sgu_b").reshape(L * 4, 128)))


def build_program(n_layers=L, dbg=None):
    nc = bass.Bass("TRN2", target_bir_lowering=False)
    dt = nc.dram_tensor
    xT_in = dt("xT", [S // T, 128, KC * T], F32, kind="ExternalInput")
    memT_in = dt("memT", [KC, 128, MEM], F32, kind="ExternalInput")
    wbig_in = dt("wbig", [L * NBIG, 128, 4096], F32, kind="ExternalInput")
    wgu_in = dt("wgu", [L * HC, 128, 2048], F32, kind="ExternalInput")
    wdr_in = dt("wdr", [L * 8, 128, HC * 128], F32, kind="ExternalInput")
    wsm_in = dt("wsm", [L, 128, 1024], F32, kind="ExternalInput")
    gains_in = dt("gains", [128, L * NG * 8], F32, kind="ExternalInput")
    smalls_in = dt("smalls", [128, L * 6], F32, kind="ExternalInput")
    sgub_in = dt("sgub", [L * 4, 128], F32, kind="ExternalInput")
    d64_in = dt("d64", [64, 128], BF16, kind="ExternalInput")
    e0_in = dt("e0", [128, 64 * 2 * 256], BF16, kind="ExternalInput")
    e1_in = dt("e1", [128, 64 * 2 * 128], BF16, kind="ExternalInput")
    cdft_in = dt("cdft", [128, 256], BF16, kind="ExternalInput")
    pb_in = dt("pb", [128, 60 * 128], BF16, kind="ExternalInput")
    outT = dt("outT", [KC, 128, OWN], F32, kind="ExternalOutput")
    wbig_b = dt("wbig_b", [L * NBIG, 128, 4096], BF16)
    wgu_b = dt("wgu_b", [L * HC, 128, 2048], BF16)
    wdr_b = dt("wdr_b", [L * 8, 128, HC * 128], BF16)
    wsm_b = dt("wsm_b", [L, 128, 1024], BF16)
    skind = "ExternalOutput" if dbg else "Internal"
    XT1 = dt("XT1", [S // T, 128, KC * T], F32, kind=skind)
    ZA = dt("ZA", [S, 256], BF16, kind=skind)
    ZC = dt("ZC", [2, S, 128], BF16, kind=skind)
    YC = dt("YC", [2, 128, S], BF16, kind=skind)

    DBA = dt("DBA", [128, 2 * 512], BF16, kind="ExternalOutput") if dbg else None
    DBB = dt("DBB", [128, 4 * 512], BF16, kind="ExternalOutput") if dbg else None
    DBU = dt("DBU", [128, 4 * 512], BF16, kind="ExternalOutput") if dbg else None

    P = Prog()
    es = ExitStack()
    with es:
        sb = lambda name, shape, dtype: es.enter_context(nc.sbuf_tensor("sb_" + name, shape, dtype))
        arena = sb("arena", [128, 58368], BF16)
        xTt = sb("xTt", [128, KC, T], F32)
        xTt2 = sb("xTt2", [128, KC, T], F32)
        yTt = sb("yTt", [128, KC, T], F32)
        uTt = sb("uTt", [128, 4, T], BF16)
        vsb = sb("vsb", [128, 2, 512], F32)
        tsb = sb("tsb", [128, 2, 512], F32)
        rstdA = sb("rstdA", [128, 512], F32)
        rstdB = sb("rstdB", [128, 512], F32)
        bsb4 = sb("bsb4", [128, 4, 128], F32)
        gains = sb("gains", [128, L * NG * 8], F32)
        smalls = sb("smalls", [128, L * 6], F32)
        stats = sb("stats", [128, 4, 8], F32)
        epst = sb("epst", [128, 2], F32)
        ssil = sb("ssil", [128, 2, 512], BF16)
        vn = sb("vn", [128, 2, 512], BF16)
        dsb = sb("dsb", [128, 2, 512], BF16)
        pbt = sb("pbt", [128, 12, 128], BF16)
        pbs = sb("pbs", [128, 12, 128], BF16)
        KTt = sb("KTt", [128, 8, MEM], BF16)
        Vt = sb("Vt", [128, 2, D], BF16)
        FTt = sb("FTt", [128, 2, 512], BF16)
        ycb = sb("ycb", [128, 2, 512], BF16)
        wsmt = sb("wsmt", [128, 1024], BF16)
        onesavg = sb("onesavg", [128, 128], BF16)
        ones1 = sb("ones1", [128, 128], BF16)
        banks = [es.enter_context(nc.psum_tensor(f"ps{i}", [128, 512], F32)) for i in range(8)]

        def av(off, n):
            return arena[:, off:off + n]
        hT = av(0, 4096).rearrange("p (k t) -> p k t", k=KC)
        sq = av(4096, 4096).rearrange("p (k t) -> p k t", k=KC)
        qT = av(8192, 4096).rearrange("p (k t) -> p k t", k=KC)
        oT = av(12288, 4096).rearrange("p (k t) -> p k t", k=KC)
        aT = av(16384, 11264).rearrange("p (k t) -> p k t", k=HC)
        ybT = av(27648, 2048).rearrange("p (k t) -> p k t", k=4)
        yaT = av(29696, 1024).rearrange("p (k t) -> p k t", k=2)
        Eat = av(30720, 2048).rearrange("p (q k t) -> p q k t", q=2, k=2)
        NBS = 3
        bigs = [av(32768 + i * 4096, 4096).rearrange("p (k c) -> p k c", k=KC) for i in range(NBS)]
        NGS = 3
        wgus = [av(45056 + i * 2048, 2048).rearrange("p (g k c) -> p g k c", g=2, k=KC) for i in range(NGS)]
        NDS = 2
        wds = [av(51200 + i * 2816, 2816).rearrange("p (k c) -> p k c", k=HC) for i in range(NDS)]
        zat = av(56832, 1536).rearrange("p (i c) -> p i c", i=6)
        d64 = arena[0:64, 57344:57472]
        cdft = av(57472, 256).rearrange("p (s t) -> p s t", s=2)
        zout = av(16384, 1024).rearrange("p (z c) -> p z c", z=2)
        Xh = av(0, 16384).rearrange("p (b c) -> p b c", b=128)
        Ah = av(16384, 16384)
        Ahv = Ah.rearrange("p (c m) -> p c m", m=128)
        NES = 2
        EKA = 8
        Eslots = [av(32768 + i * 4096, 4096) for i in range(NES)]
        GT = av(40960, 16384)
        memTt = av(0, 4096).bitcast(F32).rearrange("p (k m) -> p k m", k=KC)
        mnT = av(4096, 2048).rearrange("p (k m) -> p k m", k=KC)
        msq = av(6144, 2048).rearrange("p (k m) -> p k m", k=KC)

        bank_ctr = [0]

        held = set()

        def bank():
            while True:
                i = bank_ctr[0] % 8
                bank_ctr[0] += 1
                if i not in held:
                    return i

        def B(i):
            return banks[i]

        def G(l, gi, kc):
            c = (l * NG + gi) * 8 + kc
            return gains[:, c:c + 1]

        xbufs = [xTt, xTt2]
        C = dict(xi=0, hi=0)

        def X():
            return xbufs[C["xi"]]

        def XR(kc):
            return ("xT", C["xi"], kc)

        def XALL():
            return tuple(XR(k_) for k_ in range(KC))

        def H():
            return (hT, qT, qT)[C["hi"]]

        def SQ():
            return (sq, oT, sq)[C["hi"]]

        def HR(kc):
            return (("hT", kc), ("hT1", kc), ("qT", kc))[C["hi"]]

        def SR(kc):
            return (("sqk", kc), ("sqk1", kc), ("sqk", kc))[C["hi"]]

        def dma(eng, key, out, in_, reads, writes):
            return P.add(eng, lambda e, out=out, in_=in_: e.dma_start(out=out, in_=in_), reads, writes, dma=key)

        def mm(out, lhsT, rhs, start, stop, reads, writes):
            return P.add("pe", lambda e, o=out, l_=lhsT, r=rhs, s0=start, s1=stop:
                         e.matmul(o, l_, r, start=s0, stop=s1), reads, writes)

        def act(out, in_, func, reads, writes, scale=1.0, bias=0.0, accum=None):
            def fn(e, out=out, in_=in_, func=func, scale=scale, bias=bias, accum=accum):
                kw = {}
                if accum is not None:
                    kw["accum_out"] = accum
                return e.activation(out, in_, func, bias=bias, scale=scale, **kw)
            return P.add("act", fn, reads, writes)

        def vec(eng, fn, reads, writes):
            return P.add(eng, fn, reads, writes)

        dma("sp", "c0", gains[:, :], gains_in[:, :], (), ("gains",))
        dma("sp", "c1", smalls[:, :], smalls_in[:, :], (), ("smalls",))
        dma("sp", "c2", pbt[:, :, :], pb_in[:, 0:12 * 128].rearrange("p (s t) -> p s t", t=128), (), ("pbt",))
        vec("dve", lambda e: e.memset(onesavg[:, :], 1.0 / D), (), ("onesavg",))
        vec("dve", lambda e: e.memset(ones1[:, :], 1.0), (), ("ones1",))
        vec("dve", lambda e: e.memset(epst[:, 0:1], RMS_EPS), (), ("epst",))
        vec("dve", lambda e: e.memset(epst[:, 1:2], LN_EPS), ("epst",), ("epst",))
        GRP_A = (U_WK, U_WK + 1, U_WV, U_WV + 1)
        GRP_B = (U_U, U_V, U_WOUT, U_WOUT + 1, U_WQ, U_WQ + 1, U_WO, U_WO + 1)

        def cast_big(l, u, key):
            dma("pool", key, wbig_b[l * NBIG + u], wbig_in[l * NBIG + u], (), (("wb", l, u),))

        def cast_layer(l):
            cast_big(l, U_AZ, f"cast{l}z")
            dma("pool", f"cast{l}a", wsm_b[l], wsm_in[l], (), (("wsm", l),))
            for u in GRP_A:
                cast_big(l, u, f"cast{l}a")
            for u in GRP_B:
                cast_big(l, u, f"cast{l}b")
            for j in range(HC):
                dma("pool", f"cast{l}g", wgu_b[l * HC + j], wgu_in[l * HC + j], (), (("wgu", l, j),))
            for j in range(8):
                dma("pool", f"cast{l}d", wdr_b[l * 8 + j], wdr_in[l * 8 + j], (), (("wdr", l, j),))

        def big_deps(l, u):
            if u == U_AZ:
                return (("wb", l, U_AZ),)
            if u in GRP_A:
                return (("wsm", l),) + tuple(("wb", l, v) for v in GRP_A)
            return tuple(("wb", l, v) for v in GRP_B)
        cast_layer(0)

        class Ring:
            def __init__(self, name, slots, loader):
                self.name, self.slots, self.loader = name, slots, loader
                self.sched = []
                self.issued = 0
                self.used = 0
                self.fence = None

            def plan(self, items):
                self.sched.extend(items)

            def ensure(self, upto):
                lim = min(upto + 1, len(self.sched))
                if self.fence is not None:
                    lim = min(lim, self.fence)
                while self.issued < lim:
                    n = self.issued
                    s = n % len(self.slots)
                    self.loader(self.sched[n], self.slots[s], (self.name, s), f"{self.name}{s}")
                    self.issued += 1

            def use(self, item, ahead=None):
                n = self.used
                if self.sched[n] != item:
                    self.sched.insert(n, item)
                self.ensure(n + (len(self.slots) - 1 if ahead is None else ahead))
                self.used += 1
                s = n % len(self.slots)
                return self.slots[s], (self.name, s)

        def load_big(item, slot, res, key):
            l, u = item
            dma("sp", key, slot, wbig_b[l * NBIG + u].rearrange("p (k c) -> p k c", k=KC), big_deps(l, u), (res,))

        def load_gu(item, slot, res, key):
            l, j = item
            dma("sp", key, slot, wgu_b[l * HC + j].rearrange("p (g k c) -> p g k c", g=2, k=KC),
                tuple(("wgu", l, jj) for jj in range(HC)), (res,))

        def load_wd(item, slot, res, key):
            l, j = item
            dma("sp", key, slot, wdr_b[l * 8 + j].rearrange("p (k c) -> p k c", k=HC), tuple(("wdr", l, jj) for jj in range(8)), (res,))

        def ms_to_rstd(srcs, rstd_t, rstd_res, eps):
            n = srcs[0][0].shape[-1]
            bi = bank()
            for i, (ap, r) in enumerate(srcs):
                mm(B(bi)[:, 0:n], onesavg[:, :], ap, i == 0, i == len(srcs) - 1, (r, "onesavg"), (("ps", bi),))
            act(rstd_t[:, 0:n], B(bi)[:, 0:n], AF.Ln, (("ps", bi), "epst"), (rstd_res,), bias=epst[:, 0:1])
            act(rstd_t[:, 0:n], rstd_t[:, 0:n], AF.Exp, (rstd_res,), (rstd_res,), scale=-0.5)

        def prenorm_stats(rt=None, rres="rstdA"):
            rt = rstdA if rt is None else rt
            xa, sa = X(), SQ()
            for kc in range(KC):
                act(sa[:, kc, :], xa[:, kc, :], AF.Square, (XR(kc),), (SR(kc),))
            ms_to_rstd([(sa[:, kc, :], SR(kc)) for kc in range(KC)], rt, rres, RMS_EPS)

        def prenorm_apply(l, gi, rt=None, rres="rstdA"):
            rt = rstdA if rt is None else rt
            xa, ha = X(), H()
            for kc in range(KC):
                vec("dve", lambda e, kc=kc, xa=xa, ha=ha, rt=rt: e.scalar_tensor_tensor(
                    ha[:, kc, :], xa[:, kc, :], G(l, gi, kc), rt[:, :], ALU.mult, ALU.mult),
                    (XR(kc), rres, "gains"), (HR(kc),))

        def prenorm(l, gi, rt=None, rres="rstdA"):
            prenorm_stats(rt, rres)
            prenorm_apply(l, gi, rt, rres)

        def postnorm_stats():
            ms_to_rstd([(sq[:, kc, :], ("sqk", kc)) for kc in range(KC)], rstdB, "rstdB", RMS_EPS)

        def postnorm_apply(l, gi):
            xa = X()
            for kc in range(KC):
                vec("dve", lambda e, kc=kc: e.scalar_tensor_tensor(yTt[:, kc, :], yTt[:, kc, :], G(l, gi, kc),
                                                                 rstdB[:, :], ALU.mult, ALU.mult),
                    (("yT", kc), "rstdB", "gains"), (("yT", kc),))
                vec("dve", lambda e, kc=kc, xa=xa: e.tensor_tensor(xa[:, kc, :], xa[:, kc, :], yTt[:, kc, :], ALU.add),
                    (("yT", kc), XR(kc)), (XR(kc),))

        def postnorm_residual(l, gi):
            postnorm_stats()
            postnorm_apply(l, gi)

        def proj_to_y(ring, l, unit, rhs_list):
            for j in range(KC):
                if j % 4 == 0:
                    wap, wres = ring.use((l, unit + j // 4))
                bi = bank()
                for fc in range(KC):
                    rap, rres = rhs_list[fc]
                    mm(B(bi)[:, :], wap[:, fc, (j % 4) * 128:(j % 4 + 1) * 128], rap, fc == 0, fc == KC - 1,
                       (wres, rres), (("ps", bi),))
                act(yTt[:, j, :], B(bi)[:, :], AF.Copy, (("ps", bi),), (("yT", j),))
                act(sq[:, j, :], B(bi)[:, :], AF.Square, (("ps", bi),), (("sqk", j),))

        def kv_phase(l, bigring):
            dma("sp", "memT", memTt[:, :, :], memT_in[:, :, :].rearrange("k p m -> p k m"), (), ("memT",))
            act(msq[:, :, :], memTt[:, :, :], AF.Square, ("memT",), ("msq",))
            ms_to_rstd([(msq[:, kc, :], "msq") for kc in range(KC)], rstdA, "rstdA", RMS_EPS)
            for kc in range(KC):
                vec("dve", lambda e, kc=kc: e.scalar_tensor_tensor(mnT[:, kc, :], memTt[:, kc, :], G(l, G_MEM, kc),
                                                                 rstdA[:, 0:MEM], ALU.mult, ALU.mult),
                    ("memT", "rstdA", "gains"), ("mnT",))
            for hf in range(2):
                wap, wres = bigring.use((l, U_WK + hf))
                for jj in range(4):
                    j = hf * 4 + jj
                    bi = bank()
                    for kc in range(KC):
                        mm(B(bi)[:, 0:MEM], wap[:, kc, jj * 128:(jj + 1) * 128], mnT[:, kc, :], kc == 0, kc == KC - 1,
                           (wres, "mnT"), (("ps", bi),))
                    act(KTt[:, j, :], B(bi)[:, 0:MEM], AF.Copy, (("ps", bi),), ("KT",))
            for hf in range(2):
                wap, wres = bigring.use((l, U_WV + hf))
                for mc in range(2):
                    bi = bank()
                    for kc in range(KC):
                        mm(B(bi)[:, :], mnT[:, kc, mc * 128:(mc + 1) * 128], wap[:, kc, :], kc == 0, kc == KC - 1,
                           (wres, "mnT"), (("ps", bi),))
                    vec("dve", lambda e, bi=bi, mc=mc, hf=hf: e.tensor_copy(Vt[:, mc, hf * 512:(hf + 1) * 512], B(bi)[:, :]),
                        (("ps", bi),), ("V",))

        def load_x(l, blk):
            src = xT_in if l == 0 else XT1
            dma("sp", f"xld{C['xi']}", X()[:, :, :], src[blk].rearrange("p (k t) -> p k t", k=KC),
                (), XALL())

        def phase1_norm(l, blk):
            par = blk % 2
            C["xi"] = par
            C["hi"] = par
            prenorm(l, G_MIXPRE, rt=(rstdA, rstdB)[par], rres=("rstdA", "rstdB")[par])
            C["hi"] = 0

        def phase1_block(l, blk, azap, azres, nblk):
            par = blk % 2
            if blk == 0:
                C["xi"] = 0
                load_x(l, 0)
                if nblk > 1:
                    C["xi"] = 1
                    load_x(l, 1)
                phase1_norm(l, 0)
            if blk + 2 < nblk:
                C["xi"] = par
                load_x(l, blk + 2)
            if blk + 1 < nblk:
                phase1_norm(l, blk + 1)
            C["hi"] = par
            ha = H()
            for i in range(4):
                bi = bank()
                for kc in range(KC):
                    mm(B(bi)[:, :], ha[:, kc, i * 128:(i + 1) * 128], azap[:, kc, :], kc == 0, kc == KC - 1,
                       (HR(kc), azres), (("ps", bi),))
                zs = i % 2
                act(zout[:, zs, :], B(bi)[:, :], AF.Copy, (("ps", bi),), (("zout", zs),))
                t0 = blk * T + i * 128
                dma("sp", f"zst{zs}", ZA[t0:t0 + 128, :], zout[:, zs, 0:256], (("zout", zs),), ())
                for h in range(2):
                    dma("sp", f"zst{zs}", ZC[h, t0:t0 + 128, :], zout[:, zs, 256 + h * 128:256 + (h + 1) * 128],
                        (("zout", zs),), ())
            C["hi"] = 0

        def fft_phase(l, full):
            nkb = 128 if full else 64
            ntok = nkb * 64
            ew = 2 * nkb
            e_in = e0_in if full else e1_in
            dma("sp", "wsm", wsmt[:, :], wsm_b[l], big_deps(l, U_WK), ("wsmt",))
            dma("sp", "c3", cdft[:, :, :], cdft_in[:, :].rearrange("p (s t) -> p s t", t=128), (), ("cdft",))
            dma("sp", "c4", d64, d64_in[:, :], (), ("d64",))
            fnetw = wsmt[:, 256:512].rearrange("p (c d) -> p c d", c=2)
            for h in range(2):
                dma("sp", "xh", Xh[0:64, :, :], ZC[h].rearrange("(a b) c -> a b c", b=128), (), ("Xh",))
                for c0 in range(0, 128, 4):
                    bi = bank()
                    for cc in range(4):
                        mm(B(bi)[:, cc * 128:(cc + 1) * 128], Xh[0:64, :, c0 + cc], d64, True, True,
                           ("Xh", "d64"), (("ps", bi),))
                    src = B(bi)[:, :].rearrange("p (c m) -> p c m", c=4)
                    dst = Ahv[:, c0:c0 + 4, :]
                    if (c0 // 4) % 2 == 0:
                        vec("dve", lambda e, dst=dst, src=src: e.tensor_copy(dst, src), (("ps", bi),), ("Ah",))
                    else:
                        act(dst, src, AF.Copy, (("ps", bi),), ("Ah",))
                per_bank = 512 // ew
                for kg in range(64 // EKA):
                    es_i = kg % NES
                    eslot = Eslots[es_i][:, 0:EKA * 2 * ew].rearrange("p (a w n) -> p a w n", a=EKA, w=2)
                    dma("sp", f"els{es_i}", eslot,
                        e_in[:, kg * EKA * 2 * ew:(kg + 1) * EKA * 2 * ew].rearrange("p (a w n) -> p a w n", a=EKA, w=2),
                        (), (("E", es_i),))
                    for k0 in range(0, EKA, per_bank):
                        bi = bank()
                        for kk in range(per_bank):
                            ka = kg * EKA + k0 + kk
                            osl = B(bi)[:, kk * ew:(kk + 1) * ew]
                            mm(osl, Ahv[:, :, ka], eslot[:, k0 + kk, 0, :], True, False, ("Ah", ("E", es_i)), (("ps", bi),))
                            mm(osl, Ahv[:, :, 64 + ka], eslot[:, k0 + kk, 1, :], False, True, ("Ah", ("E", es_i)), (("ps", bi),))
                        for kk in range(per_bank):
                            ka = kg * EKA + k0 + kk
                            src = B(bi)[:, kk * ew:(kk + 1) * ew].rearrange("p (r b) -> p r b", r=2)
                            dst = GT[:, 0:2 * ntok].rearrange("p (r a b) -> p r a b", r=2, a=64)[:, :, ka, :]
                            if kk % 2 == 0:
                                vec("dve", lambda e, dst=dst, src=src: e.tensor_copy(dst, src), (("ps", bi),), ("GT",))
                            else:
                                act(dst, src, AF.Copy, (("ps", bi),), ("GT",))
                GTp = GT[:, 0:2 * ntok].rearrange("p (r a b) -> p r a b", r=2, a=64)
                for cb in range(ntok // 512):
                    bi = bank()
                    mm(B(bi)[:, :], cdft[:, 0, :], GTp[:, 0, :, cb * 8:(cb + 1) * 8].rearrange("p a b -> p b a"), True, False,
                       ("GT", "cdft"), (("ps", bi),))
                    mm(B(bi)[:, :], cdft[:, 1, :], GTp[:, 1, :, cb * 8:(cb + 1) * 8].rearrange("p a b -> p b a"), False, True,
                       ("GT", "cdft"), (("ps", bi),))
                    fs = cb % 2
                    act(FTt[:, fs, :], B(bi)[:, :], AF.Copy, (("ps", bi),), (("FT", fs),), scale=FFT_SCALE)
                    b2 = bank()
                    mm(B(b2)[:, :], fnetw[:, h, :], FTt[:, fs, :], True, True, (("FT", fs), "wsmt"), (("ps", b2),))
                    vec("dve", lambda e, b2=b2, fs=fs: e.tensor_copy(ycb[:, fs, :], B(b2)[:, :]), (("ps", b2),), (("ycs", fs),))
                    dma("sp", f"ycst{fs}", YC[h, :, cb * 512:(cb + 1) * 512], ycb[:, fs, :], (("ycs", fs),), ())

        def dbg_dump(blk):
            if not dbg:
                return
            if blk == 0:
                dma("sp", "dbd", DBA[:, :].rearrange("p (k t) -> p k t", k=2), yaT, tuple(("yaT", c_) for c_ in range(2)), ())
                dma("sp", "dbd", DBB[:, :].rearrange("p (k t) -> p k t", k=4), ybT, tuple(("ybT", c_) for c_ in range(4)), ())
                dma("sp", "dbd", DBU[:, :].rearrange("p (k t) -> p k t", k=4), uTt[:, :, :], tuple(("uT", c_) for c_ in range(4)), ())

        def phase2_loads(l, blk):
            C["xi"] = blk % 2
            C["hi"] = 0
            load_x(l, blk)
            dma("sp", "ycld", ycb[:, :, :], YC[:, :, blk * T:(blk + 1) * T].rearrange("h p t -> p h t"),
                (), ("ycb",))

        def phase2_loads_za(l, blk):
            t0 = blk * 4
            tp, tn = (t0 - 1) % 64, (t0 + 4) % 64
            dma("sp", "zald0", zat[:, 0, :], ZA[tp * 128:(tp + 1) * 128, :], (), (("zat", 0),))
            dma("sp", "zald1", zat[:, 1:5, :], ZA[t0 * 128:(t0 + 4) * 128, :].rearrange("(i p) c -> p i c", p=128),
                (), (("zat", 1),))
            dma("sp", "zald2", zat[:, 5, :], ZA[tn * 128:(tn + 1) * 128, :], (), (("zat", 2),))
            spec = [(i, {0: 1, 31: 2, 32: 3, 63: 4}[t0 + i]) for i in range(4) if (t0 + i) in (0, 31, 32, 63)]
            if spec:
                pset_ = spec[0][1]
                dma("sp", "pbs", pbs[:, :, :], pb_in[:, pset_ * 12 * 128:(pset_ + 1) * 12 * 128].rearrange("p (s t) -> p s t", t=128),
                    (), ("pbs",))

        def phase2_head(l, blk):
            xi_save = C["xi"]
            C["xi"] = blk % 2
            C["hi"] = 2
            prenorm(l, G_MIXPRE)
            C["hi"] = 0
            C["xi"] = xi_save

        def make_fe(l, blk, bigring):
            poolw = wsmt[:, 0:256].rearrange("p (c d) -> p c d", c=2)
            sguw = wsmt[:, 512:1024].rearrange("p (h q) -> p h q", h=4)
            t0 = blk * 4
            st_ = {}

            def zv_tile(i):
                bi = bank()
                for kc in range(KC):
                    mm(B(bi)[:, :], qT[:, kc, i * 128:(i + 1) * 128], st_["vwap"][:, kc, :], kc == 0, kc == KC - 1,
                       (("qT", kc), st_["vwres"]), (("ps", bi),))
                st = stats[:, i, :]
                vs = i % 2
                act(vsb[:, vs, :], B(bi)[:, :], AF.Copy, (("ps", bi),), (("vsb", vs),))
                act(FTt[:, vs, :], B(bi)[:, :], AF.Square, (("ps", bi),), (("vsq", vs),))
                vec("dve", lambda e, st=st, vs=vs: e.reduce_sum(st[:, 0:1], vsb[:, vs, :], mybir.AxisListType.X),
                    (("vsb", vs),), (("st", i),))
                vec("dve", lambda e, st=st, vs=vs: e.reduce_sum(st[:, 1:2], FTt[:, vs, :], mybir.AxisListType.X),
                    (("vsq", vs), ("st", i)), (("st", i),))
                vec("dve", lambda e, st=st: e.tensor_scalar(st[:, 2:3], st[:, 0:1], 1.0 / 512, None, ALU.mult),
                    (("st", i),), (("st", i),))
                vec("dve", lambda e, st=st: e.tensor_tensor(st[:, 3:4], st[:, 2:3], st[:, 2:3], ALU.mult),
                    (("st", i),), (("st", i),))
                vec("dve", lambda e, st=st: e.scalar_tensor_tensor(st[:, 4:5], st[:, 1:2], 1.0 / 512, st[:, 3:4],
                                                                 ALU.mult, ALU.subtract), (("st", i),), (("st", i),))
                act(st[:, 5:6], st[:, 4:5], AF.Ln, (("st", i), "epst"), (("st", i),), bias=epst[:, 1:2])
                act(st[:, 5:6], st[:, 5:6], AF.Exp, (("st", i),), (("st", i),), scale=-0.5)
                vec("dve", lambda e, st=st, vs=vs: e.tensor_scalar(vn[:, vs, :], vsb[:, vs, :], st[:, 2:3], st[:, 5:6],
                                                                 ALU.subtract, ALU.mult),
                    (("vsb", vs), ("st", i)), (("vn", vs),))

            def sgu_tile(i):
                vs = i % 2
                for h in range(4):
                    mm(B(st_["sgb"][h])[:, i * 128:(i + 1) * 128], vn[:, vs, h * 128:(h + 1) * 128], sguw[:, h, :], True, True,
                       (("vn", vs), "wsmt"), (("ps", st_["sgb"][h]),))

            def pool_chunk(ch):
                pbk = [bank(), bank()]
                for gg in range(2):
                    g = 2 * ch + gg
                    for i in range(4):
                        ti = t0 + i
                        ptile = pbs if ti in (0, 31, 32, 63) else pbt
                        for r in range(3):
                            mm(B(pbk[gg])[:, i * 128:(i + 1) * 128], zat[:, i + r, ch * 128:(ch + 1) * 128],
                               ptile[:, r * 4 + g, :], r == 0, r == 2,
                               (("zat", 0), ("zat", 1), ("zat", 2), "pbt", "pbs"), (("ps", pbk[gg]),))
                vec("dve", lambda e, ch=ch, b0=pbk[0]: e.tensor_copy(dsb[0:64, ch, :], B(b0)[0:64, :]),
                    (("ps", pbk[0]),), (("dsb", ch, 0),))
                act(dsb[64:128, ch, :], B(pbk[1])[64:128, :], AF.Copy, (("ps", pbk[1]),), (("dsb", ch, 1),))

            def pool_out(ch):
                bi = bank()
                mm(B(bi)[:, :], poolw[:, ch, :], dsb[:, ch, :], True, True, (("dsb", ch, 0), ("dsb", ch, 1), "wsmt"), (("ps", bi),))
                vec("dve", lambda e, bi=bi, ch=ch: e.tensor_scalar(yaT[:, ch, :], B(bi)[:, :],
                                                                    smalls[:, l * 6 + ch: l * 6 + ch + 1], None, ALU.mult),
                    (("ps", bi), "smalls"), (("yaT", ch),))

            def part1():
                st_["vwap"], st_["vwres"] = bigring.use((l, U_V))
                st_["sgb"] = [bank() for _ in range(4)]
                held.update(st_["sgb"])
                zv_tile(0)
                zv_tile(1)
                uwap, uwres = bigring.use((l, U_U), ahead=1)
                for j in range(4):
                    bi = bank()
                    for kc in range(KC):
                        mm(B(bi)[:, :], uwap[:, kc, j * 128:(j + 1) * 128], qT[:, kc, :], kc == 0, kc == KC - 1,
                           (uwres, ("qT", kc)), (("ps", bi),))
                    act(uTt[:, j, :], B(bi)[:, :], AF.Copy, (("ps", bi),), (("uT", j),))
                sgu_tile(0)
                zv_tile(2)
                sgu_tile(1)

            def part2():
                part2_body()

            def part2_body():
              sgb = st_["sgb"]
              if True:
                zv_tile(3)
                sgu_tile(2)
                pool_out(0)
                pool_out(1)
                sgu_tile(3)
                for h in range(4):
                    ts_ = h % 2
                    sc = smalls[:, l * 6 + 2 + h: l * 6 + 3 + h]
                    bsl = bsb4[:, h, :]
                    bb = bass.AP(tensor=bsl.tensor, offset=bsl.offset, ap=[list(bsl.ap[0]), [0, 4], [1, 128]])
                    vec("dve", lambda e, h=h, ts_=ts_, sc=sc, bb=bb: e.scalar_tensor_tensor(
                        tsb[:, ts_, :].rearrange("p (i q) -> p i q", i=4), B(sgb[h])[:, :].rearrange("p (i q) -> p i q", i=4), sc, bb,
                        ALU.mult, ALU.add), (("ps", sgb[h]), "smalls") + tuple(("bsb4", h_) for h_ in range(4)), (("tsb", ts_),))
                    vec("pool", lambda e, h=h, ts_=ts_: e.tensor_tensor(ybT[:, h, :], tsb[:, ts_, :], uTt[:, h, :], ALU.mult),
                        (("tsb", ts_), ("uT", h)), (("ybT", h),))
                held.difference_update(sgb)

            def part0():
                pool_chunk(0)
                pool_chunk(1)

            return part0, part1, part2

        def phase2_body1(l, blk, bigring, guring, nb2):
            C["xi"] = blk % 2
            C["hi"] = 0
            cat = [(yaT[:, 0, :], ("yaT", 0)), (yaT[:, 1, :], ("yaT", 1))] + \
                  [(ybT[:, h, :], ("ybT", h)) for h in range(4)] + \
                  [(ycb[:, 0, :], "ycb"), (ycb[:, 1, :], "ycb")]
            nxt = blk + 1 < nb2
            if nxt:
                phase2_loads_za(l, blk + 1)
                fe0, fe1, fe2 = make_fe(l, blk + 1, bigring)
            proj_to_y(bigring, l, U_WOUT, cat)
            postnorm_residual(l, G_MIXPOST)
            if nxt:
                fe0()
            if blk + 1 < nb2:
                phase2_loads(l, blk + 1)
                C["xi"] = blk % 2
            prenorm(l, G_XAPRE)
            for hf in range(2):
                wap, wres = bigring.use((l, U_WQ + hf))
                for jj in range(4):
                    j = hf * 4 + jj
                    bi = bank()
                    for kc in range(KC):
                        mm(B(bi)[:, :], wap[:, kc, jj * 128:(jj + 1) * 128], hT[:, kc, :], kc == 0, kc == KC - 1,
                           (wres, ("hT", kc)), (("ps", bi),))
                    act(qT[:, j, :], B(bi)[:, :], AF.Copy, (("ps", bi),), (("qT", j),), scale=1.0 / 16.0)
            for h in range(4):
                for mc in range(2):
                    bi = bank()
                    for dc in range(2):
                        mm(B(bi)[:, :], KTt[:, 2 * h + dc, mc * 128:(mc + 1) * 128], qT[:, 2 * h + dc, :], dc == 0, dc == 1,
                           ("KT", ("qT", 2 * h + dc)), (("ps", bi),))
                    act(Eat[:, h % 2, mc, :], B(bi)[:, :], AF.Exp, (("ps", bi),), (("Eat", h % 2, mc),))
                bd = bank()
                for mc in range(2):
                    mm(B(bd)[:, :], ones1[:, :], Eat[:, h % 2, mc, :], mc == 0, mc == 1, (("Eat", h % 2, mc), "ones1"), (("ps", bd),))
                act(tsb[:, h % 2, :], B(bd)[:, :], AF.Ln, (("ps", bd),), (("tsb", h % 2),))
                act(tsb[:, h % 2, :], tsb[:, h % 2, :], AF.Exp, (("tsb", h % 2),), (("tsb", h % 2),), scale=-1.0)
                for dc in range(2):
                    bi = bank()
                    for mc in range(2):
                        mm(B(bi)[:, :], Vt[:, mc, (2 * h + dc) * 128:(2 * h + dc + 1) * 128], Eat[:, h % 2, mc, :], mc == 0, mc == 1,
                           ("V", ("Eat", h % 2, mc)), (("ps", bi),))
                    vec("dve", lambda e, bi=bi, h=h, dc=dc: e.tensor_tensor(oT[:, 2 * h + dc, :], B(bi)[:, :], tsb[:, h % 2, :], ALU.mult),
                        (("ps", bi), ("tsb", h % 2)), (("oT", 2 * h + dc),))
            if nxt:
                phase2_head(l, blk + 1)
            proj_to_y(bigring, l, U_WO, [(oT[:, j, :], ("oT", j)) for j in range(KC)])
            postnorm_stats()
            if nxt:
                fe1()
            postnorm_apply(l, G_XAPOST)
            prenorm_stats()
            prenorm_apply(l, G_FFNPRE)
            if nxt:
                fe2()
            for j in range(HC):
                wap, wres = guring.use((l, j))
                bg, bu = bank(), bank()
                for kc in range(KC):
                    mm(B(bg)[:, :], wap[:, 0, kc, :], hT[:, kc, :], kc == 0, kc == KC - 1, (wres, ("hT", kc)), (("ps", bg),))
                for kc in range(KC):
                    mm(B(bu)[:, :], wap[:, 1, kc, :], hT[:, kc, :], kc == 0, kc == KC - 1, (wres, ("hT", kc)), (("ps", bu),))
                ss = j % 2
                act(ssil[:, ss, :], B(bg)[:, :], AF.Silu, (("ps", bg),), (("ssil", ss),))
                vec("dve", lambda e, bu=bu, ss=ss, j=j: e.tensor_tensor(aT[:, j, :], B(bu)[:, :], ssil[:, ss, :], ALU.mult),
                    (("ps", bu), ("ssil", ss)), (("aT", j),))

        def phase2_body2(l, blk, wdring, last):
            C["xi"] = blk % 2
            C["hi"] = 0
            bigring.ensure(bigring.used + 1)
            for j in range(KC):
                wap, wres = wdring.use((l, j))
                bi = bank()
                for hc in range(HC):
                    mm(B(bi)[:, :], wap[:, hc, :], aT[:, hc, :], hc == 0, hc == HC - 1, (wres, ("aT", hc)), (("ps", bi),))
                act(yTt[:, j, :], B(bi)[:, :], AF.Copy, (("ps", bi),), (("yT", j),))
                act(sq[:, j, :], B(bi)[:, :], AF.Square, (("ps", bi),), (("sqk", j),))
            postnorm_residual(l, G_FFNPOST)
            if last:
                dma("sp", f"xst{C['xi']}", outT[:, :, blk * T:(blk + 1) * T].rearrange("k p t -> p k t"), X()[:, :, :],
                    XALL(), ())
            else:
                dma("sp", f"xst{C['xi']}", XT1[blk].rearrange("p (k t) -> p k t", k=KC), X()[:, :, :], XALL(), ())

        bigring = Ring("big", [(b_, None) for b_ in bigs], None)
        guring = Ring("gu", [(g_, None) for g_ in wgus], None)
        wdring = Ring("wd", [(w_, None) for w_ in wds], None)
        bigring.slots = bigs
        bigring.loader = load_big
        guring.slots = wgus
        guring.loader = load_gu
        wdring.slots = wds
        wdring.loader = load_wd

        nblk_all = S // T
        nblk_own = OWN // T
        p2start, guend, wdend = {}, {}, {}
        for l in range(n_layers):
            nb2 = nblk_all if l < L - 1 else nblk_own
            bigring.plan([(l, U_AZ)])
            bigring.plan([(l, U_WK), (l, U_WK + 1), (l, U_WV), (l, U_WV + 1)])
            p2start[l] = len(bigring.sched)
            bigring.plan([(l, U_V), (l, U_U)])
            for blk in range(nb2):
                bigring.plan([(l, U_WOUT), (l, U_WOUT + 1), (l, U_WQ), (l, U_WQ + 1), (l, U_WO), (l, U_WO + 1)])
                if blk + 1 < nb2:
                    bigring.plan([(l, U_V), (l, U_U)])
                guring.plan([(l, j) for j in range(HC)])
                wdring.plan([(l, j) for j in range(KC)])
            guend[l] = len(guring.sched)
            wdend[l] = len(wdring.sched)

        def nop_fn(e):
            return e.nop()

        stages = ["casts", "p1", "kv", "fft", "p2"]
        dstage = stages.index(dbg.split(":")[0]) if dbg else 99
        dnblk = int(dbg.split(":")[1]) if (dbg and ":" in dbg) else None
        for l in range(n_layers):
            full = l < L - 1
            nb2 = nblk_all if full else nblk_own
            if dbg and (l > 0 or dstage < 1):
                break
            bigring.fence = p2start[l]
            guring.fence = guend[l - 1] if l > 0 else 0
            wdring.fence = wdend[l - 1] if l > 0 else 0
            azap, azres = bigring.use((l, U_AZ))
            for blk in range(nblk_all):
                phase1_block(l, blk, azap, azres, nblk_all)
            P.barrier("sp", nop_fn)
            if dstage < 2:
                break
            kv_phase(l, bigring)
            P.barrier("sp", nop_fn)
            if dstage < 3:
                break
            fft_phase(l, full)
            P.barrier("sp", nop_fn)
            if l + 1 < n_layers and not dbg:
                cast_layer(l + 1)
            if dstage < 4:
                break
            bigring.fence = p2start.get(l + 1)
            guring.fence = guend[l]
            wdring.fence = wdend[l]
            if dnblk is not None:
                nb2 = dnblk
            for h in range(4):
                dma("sp", "bsb", bsb4[:, h, :], sgub_in[l * 4 + h:l * 4 + h + 1, :].partition_broadcast(128),
                    (), (("bsb4", h),))
            phase2_loads(l, 0)
            phase2_loads_za(l, 0)
            C["xi"] = 0
            phase2_head(l, 0)
            fe0, fe1, fe2 = make_fe(l, 0, bigring)
            fe1()
            fe0()
            fe2()
            for blk in range(nb2):
                phase2_body1(l, blk, bigring, guring, nb2)
                phase2_body2(l, blk, wdring, last=(l == L - 1))
            P.barrier("sp", nop_fn)

        P.barrier("sp", nop_fn)
        keys = P.finalize()
        sems = {}
        for k in keys:
            sems[k] = es.enter_context(nc.semaphore(("s_" + str(k[1])).replace(" ", "")[:24]))
        with nc.Block() as block:
            @block.tensor
            def _(e):
                P.emit_engine("pe", e, sems)

            @block.scalar
            def _(e):
                P.emit_engine("act", e, sems)

            @block.vector
            def _(e):
                P.emit_engine("dve", e, sems)

            @block.gpsimd
            def _(e):
                P.emit_engine("pool", e, sems)

            @block.sync
            def _(e):
                P.emit_engine("sp", e, sems)
    return nc


_CACHE = {}


def kernel(**inputs):
    x = np.asarray(inputs["x"], dtype=np.float32)
    mem = np.asarray(inputs["mem"], dtype=np.float32)
    wts = _prep_weights(inputs)
    consts = [_const_tables(0), _const_tables(1)]
    in_maps = []
    for c in range(8):
        b, half = c // 2, c % 2
        xl = np.roll(x[b], -OWN * half, axis=0)
        m = dict(wts)
        m.update(consts[half])
        m["xT"] = np.ascontiguousarray(xl.T.reshape(KC, 128, S // T, T).transpose(2, 1, 0, 3)).reshape(S // T, 128, KC * T)
        m["memT"] = np.ascontiguousarray(mem[b].T).reshape(KC, 128, MEM)
        in_maps.append(m)
    if "nc" not in _CACHE:
        _CACHE["nc"] = build_program()
    res = run_bass_kernel_spmd(_CACHE["nc"], in_maps, core_ids=list(range(8)))
    out = np.empty((4, S, D), np.float32)
    for c in range(8):
        b, half = c // 2, c % 2
        oT = np.asarray(res.results[c]["outT"]).reshape(D, OWN)
        out[b, half * OWN:(half + 1) * OWN, :] = oT.T
    return out
```
